# Optimizing a Trainium2 kernel written in Bass

```python
import math
import jax, jax.numpy as jnp
from jax import lax
import numpy as np

D_MODEL = 1024
BATCH = 8
SEQ = 2048
DEPTH = 1

CHUNK = 64
GM_WIDTH = 1024
GM_BLOCK = 128
GM_GROUPS = 8
GM_GROUP_DIM = GM_WIDTH // GM_GROUPS
RW_WIDTH = 1024
RW_HEAD_DIM = 64
RW_HEADS = RW_WIDTH // RW_HEAD_DIM
RW_DECAY_RANK = 64
RW_ICLR_RANK = 64
DECAY_SCALE = math.exp(-0.5)
GM_COLS = 3 * GM_WIDTH
RW_COLS = 4 * RW_WIDTH + RW_DECAY_RANK + RW_ICLR_RANK
GATE_COLS = 2 * D_MODEL
IN_COLS = GM_COLS + RW_COLS + GATE_COLS
RMS_EPS = 1e-6
LN_EPS = 1e-5
GN_EPS = 64e-5
L2_EPS = 1e-12

kernel_name = "hybrid_gmlp_rwkv7_gated_block"


def rms_norm(x, g):
    xf = x.astype(jnp.float32)
    y = xf * lax.rsqrt(jnp.mean(xf * xf, axis=-1, keepdims=True) + RMS_EPS)
    return (y * g.astype(jnp.float32)).astype(x.dtype)


def layer_norm(x, g, b):
    xf = x.astype(jnp.float32)
    mu = jnp.mean(xf, axis=-1, keepdims=True)
    var = jnp.mean(jnp.square(xf - mu), axis=-1, keepdims=True)
    y = (xf - mu) * lax.rsqrt(var + LN_EPS) * g.astype(jnp.float32) + b.astype(jnp.float32)
    return y.astype(x.dtype)


def token_shift(p):
    return jnp.pad(p, ((0, 0), (1, 0), (0, 0)))[:, :-1]


def gmlp_branch(pa, ln_g, ln_b, w_s, b_s):
    B, S, _ = pa.shape
    u = jax.nn.gelu(pa[..., :GM_WIDTH])
    v = jax.nn.gelu(pa[..., GM_WIDTH:2 * GM_WIDTH])
    z = pa[..., 2 * GM_WIDTH:]
    v = layer_norm(v, ln_g, ln_b)
    nb = S // GM_BLOCK
    v = v.reshape(B, nb, GM_BLOCK, GM_GROUPS, GM_GROUP_DIM)
    chunk_id = jnp.arange(GM_BLOCK) // CHUNK
    mask = chunk_id[:, None] >= chunk_id[None, :]
    w = jnp.where(mask[None], w_s, 0.0)
    s = jnp.einsum('gij,bnjgc->bnigc', w, v) + b_s.T[None, None, :, :, None]
    s = s.reshape(B, S, GM_WIDTH)
    return u * s * jax.nn.silu(z)


def wkv7_scan(r, w, k, v, kk, a):
    B, S, H, N = r.shape

    def step(state, inp):
        r_t, w_t, k_t, v_t, kk_t, a_t = inp
        sa = jnp.einsum('bhvk,bhk->bhv', state, -kk_t)
        state = (state * w_t[:, :, None, :]
                 + sa[..., None] * (kk_t * a_t)[:, :, None, :]
                 + v_t[..., None] * k_t[:, :, None, :])
        o_t = jnp.einsum('bhvk,bhk->bhv', state, r_t)
        return state, o_t

    xs = (jnp.moveaxis(r, 1, 0), jnp.moveaxis(w, 1, 0), jnp.moveaxis(k, 1, 0),
          jnp.moveaxis(v, 1, 0), jnp.moveaxis(kk, 1, 0), jnp.moveaxis(a, 1, 0))
    state0 = jnp.zeros((B, H, N, N), jnp.float32)
    _, o = lax.scan(step, state0, xs)
    return jnp.moveaxis(o, 0, 1)


def rwkv7_branch(pb, mu, w0, decay_up, a0, iclr_up, k_k, k_a, r_k, gn_g, gn_b):
    B, S, _ = pb.shape
    f32 = jnp.float32
    p = pb.astype(f32)
    p = p + mu.astype(f32) * (token_shift(p) - p)
    W = RW_WIDTH
    r = p[..., :W]
    k = p[..., W:2 * W]
    v = p[..., 2 * W:3 * W]
    z = p[..., 3 * W:4 * W]
    wd = p[..., 4 * W:4 * W + RW_DECAY_RANK]
    ad = p[..., 4 * W + RW_DECAY_RANK:]
    decay = jnp.exp(-DECAY_SCALE * jax.nn.sigmoid(w0.astype(f32) + jnp.tanh(wd) @ decay_up.astype(f32)))
    iclr = jax.nn.sigmoid(a0.astype(f32) + ad @ iclr_up.astype(f32))

    def heads(t):
        return t.reshape(B, S, RW_HEADS, RW_HEAD_DIM)

    kk = heads(k * k_k.astype(f32))
    kk = kk * lax.rsqrt(jnp.maximum(jnp.sum(kk * kk, axis=-1, keepdims=True), L2_EPS))
    k = k * (1.0 + (iclr - 1.0) * k_a.astype(f32))
    r_h, k_h, v_h, w_h, a_h = heads(r), heads(k), heads(v), heads(decay), heads(iclr)
    o = wkv7_scan(r_h, w_h, k_h, v_h, kk, a_h)
    om = jnp.mean(o, axis=-1, keepdims=True)
    ov = jnp.mean(jnp.square(o - om), axis=-1, keepdims=True)
    o = ((o - om) * lax.rsqrt(ov + GN_EPS)).reshape(B, S, W) * gn_g.astype(f32) + gn_b.astype(f32)
    bonus = jnp.sum(r_h * k_h * r_k.astype(f32), axis=-1, keepdims=True) * v_h
    o = o + bonus.reshape(B, S, W)
    return (o * jax.nn.silu(z)).astype(pb.dtype)


def setup_inputs(seed: int = 0) -> dict:
    key = jax.random.key(seed)
    ks = jax.random.split(key, 24)
    L, D = DEPTH, D_MODEL
    nrm = jax.random.normal
    uni = jax.random.uniform
    return {
        "x": nrm(ks[0], (BATCH, SEQ, D), jnp.float32),
        "norm_pre_g": 1.0 + 0.05 * nrm(ks[1], (L, D), jnp.float32),
        "w_in": nrm(ks[2], (L, D, IN_COLS), jnp.float32) * D ** -0.5,
        "gm_ln_g": 1.0 + 0.05 * nrm(ks[3], (L, GM_WIDTH), jnp.float32),
        "gm_ln_b": 0.02 * nrm(ks[4], (L, GM_WIDTH), jnp.float32),
        "gm_w_s": nrm(ks[5], (L, GM_GROUPS, GM_BLOCK, GM_BLOCK), jnp.float32) * GM_BLOCK ** -0.5,
        "gm_b_s": 1.0 + 0.1 * nrm(ks[6], (L, GM_GROUPS, GM_BLOCK), jnp.float32),
        "rw_mu": uni(ks[7], (L, RW_COLS), jnp.float32),
        "rw_w0": uni(ks[8], (L, RW_WIDTH), jnp.float32, -3.0, 3.0),
        "rw_decay_up": 0.1 * nrm(ks[9], (L, RW_DECAY_RANK, RW_WIDTH), jnp.float32),
        "rw_a0": 0.1 * nrm(ks[10], (L, RW_WIDTH), jnp.float32),
        "rw_iclr_up": 0.1 * nrm(ks[11], (L, RW_ICLR_RANK, RW_WIDTH), jnp.float32),
        "rw_k_k": 0.85 + 0.1 * nrm(ks[12], (L, RW_WIDTH), jnp.float32),
        "rw_k_a": 1.0 + 0.1 * nrm(ks[13], (L, RW_WIDTH), jnp.float32),
        "rw_r_k": 0.1 * nrm(ks[14], (L, RW_HEADS, RW_HEAD_DIM), jnp.float32),
        "rw_gn_g": 1.0 + 0.05 * nrm(ks[15], (L, RW_WIDTH), jnp.float32),
        "rw_gn_b": 0.02 * nrm(ks[16], (L, RW_WIDTH), jnp.float32),
        "w_branch_a": nrm(ks[17], (L, GM_WIDTH, D), jnp.float32) * GM_WIDTH ** -0.5,
        "w_branch_b": nrm(ks[18], (L, RW_WIDTH, D), jnp.float32) * RW_WIDTH ** -0.5,
        "w_out": nrm(ks[19], (L, D, D), jnp.float32) * D ** -0.5,
        "norm_post_g": 1.0 + 0.05 * nrm(ks[20], (L, D), jnp.float32),
    }


def reference(x, norm_pre_g, w_in, gm_ln_g, gm_ln_b, gm_w_s, gm_b_s, rw_mu, rw_w0,
              rw_decay_up, rw_a0, rw_iclr_up, rw_k_k, rw_k_a, rw_r_k, rw_gn_g, rw_gn_b,
              w_branch_a, w_branch_b, w_out, norm_post_g):
    for l in range(DEPTH):
        h = rms_norm(x, norm_pre_g[l])
        p = h @ w_in[l]
        pa = p[..., :GM_COLS]
        pb = p[..., GM_COLS:GM_COLS + RW_COLS]
        pg = p[..., GM_COLS + RW_COLS:]
        ya = gmlp_branch(pa, gm_ln_g[l], gm_ln_b[l], gm_w_s[l], gm_b_s[l])
        yb = rwkv7_branch(pb, rw_mu[l], rw_w0[l], rw_decay_up[l], rw_a0[l], rw_iclr_up[l],
                          rw_k_k[l], rw_k_a[l], rw_r_k[l], rw_gn_g[l], rw_gn_b[l])
        gate_a = jax.nn.sigmoid(pg[..., :D_MODEL])
        gate_b = jax.nn.sigmoid(pg[..., D_MODEL:])
        merged = gate_a * (ya @ w_branch_a[l]) + gate_b * (yb @ w_branch_b[l])
        x = x + rms_norm(merged @ w_out[l], norm_post_g[l])
    return x
```

```python
import math
import os
from contextlib import ExitStack
import numpy as np
import concourse.bass as bass
import concourse.mybir as mybir
from concourse.bass_utils import run_bass_kernel_spmd

F32 = mybir.dt.float32
BF16 = mybir.dt.bfloat16
AF = mybir.ActivationFunctionType
ALU = mybir.AluOpType

T = 2048
D = 1024
NCOL = 9344
DSCALE = math.exp(-0.5)
PB0 = 3072
PG0 = 7296


class Region:
    __slots__ = ("name", "w", "r", "excl")

    def __init__(self, name, excl=False):
        self.name = name
        self.w = None
        self.r = []
        self.excl = excl


class Sched:
    ROLL = 30000

    def __init__(self, nc):
        self.nc = nc
        self.eng = {"pe": nc.tensor, "act": nc.scalar, "dve": nc.vector,
                    "pool": nc.gpsimd, "sp": nc.sync}
        self.sem = {}
        self.cnt = {}
        self.nsem = 0
        for e in self.eng:
            self._newsem(e)
        self.waited = {e: {} for e in self.eng}
        self.dslots = {}
        for q in ("sp", "pool"):
            self.dslots[q] = [[self._alloc(f"d_{q}_{i}"), 0] for i in range(8)]
        self.dnext = {"sp": 0, "pool": 0}
        self.ninst = {e: 0 for e in self.eng}

    def _alloc(self, name):
        self.nsem += 1
        return self.nc.alloc_semaphore(name)

    def _newsem(self, e):
        self.sem[e] = self._alloc(f"s_{e}_{self.nsem}")
        self.cnt[e] = 0

    def _need(self, e, tok, is_war=False, for_dma=False):
        if tok is None:
            return
        sem, val, pe, kind = tok
        if kind == "c" and pe == e and not for_dma:
            if e == "pe" or is_war:
                return
        key = sem.num
        if self.waited[e].get(key, 0) >= val:
            return
        self.eng[e].wait_ge(sem, val)
        self.waited[e][key] = val

    def _deps(self, e, reads, writes, for_dma=False):
        for R in reads:
            self._need(e, R.w, for_dma=for_dma)
            if R.excl:
                for t in R.r:
                    if t[2] != e:
                        self._need(e, t)
        for R in writes:
            self._need(e, R.w, for_dma=for_dma)
            for t in R.r:
                self._need(e, t, is_war=True, for_dma=for_dma)

    def _commit(self, tok, reads, writes):
        for R in reads:
            if tok[3] == "c":
                R.r = [t for t in R.r if not (t[3] == "c" and t[2] == tok[2])]
            R.r.append(tok)
        for R in writes:
            R.w = tok
            R.r = []

    def op(self, e, fn, reads=(), writes=(), sig=True):
        self._deps(e, reads, writes)
        inst = fn()
        self.ninst[e] += 1
        if not sig:
            tok = (self.sem[e], self.cnt[e] + 1, e, "c")
            self._commit(tok, reads, writes)
            return tok
        self.cnt[e] += 1
        inst.then_inc(self.sem[e], 1)
        tok = (self.sem[e], self.cnt[e], e, "c")
        self._commit(tok, reads, writes)
        if self.cnt[e] >= self.ROLL:
            self._newsem(e)
        return tok

    def dma(self, q, out, in_, reads=(), writes=()):
        self._deps(q, reads, writes, for_dma=True)
        i = self.dnext[q]
        self.dnext[q] = (i + 1) % len(self.dslots[q])
        slot = self.dslots[q][i]
        if slot[1] > 0:
            self._need(q, (slot[0], slot[1], q, "d"))
        slot[1] += 16
        self.eng[q].dma_start(out=out, in_=in_).then_inc(slot[0], 16)
        tok = (slot[0], slot[1], q, "d")
        self._commit(tok, reads, writes)
        return tok

    def wait_tok(self, e, tok):
        self._need(e, tok)

    def barrier(self):
        engs = list(self.eng)
        snap = {e: (self.sem[e], self.cnt[e]) for e in engs}
        dsn = [(sl[0], sl[1], q) for q in self.dslots for sl in self.dslots[q] if sl[1] > 0]
        for e in engs:
            for e2 in engs:
                if snap[e2][1] > 0:
                    self._need(e, (snap[e2][0], snap[e2][1], e2, "c"), for_dma=True)
            for (sem, val, q) in dsn:
                self._need(e, (sem, val, q, "d"))


class _Stop(Exception):
    pass


def build_program(stop=None):
    nc = bass.Bass("TRN2", target_bir_lowering=False)
    S = Sched(nc)
    try:
        _build(nc, S, stop)
    except _Stop:
        S.barrier()
    return nc, S


def _build(nc, S, stop):
    _dumps = os.environ.get("DUMP", "").split(",")

    def dump(name, tile, regions):
        if name not in _dumps:
            return
        shp = list(tile.shape)
        dt_ = tile.dtype
        d = nc.dram_tensor("dbg_" + name, shp, dt_, kind="ExternalOutput").ap()
        S.dma("sp", d, tile[:] if not isinstance(tile, bass.AP) else tile, list(regions), [])

    def chk(name):
        if stop == name:
            raise _Stop()

    def din(name, shape):
        return nc.dram_tensor(name, shape, F32, kind="ExternalInput").ap()

    x = din("x", [T, D])
    g_pre = din("norm_pre_g", [1, D])
    w_in = din("w_in", [D, NCOL])
    ln_g = din("gm_ln_g", [1, D])
    ln_b = din("gm_ln_b", [1, D])
    w_s = din("gm_w_s", [8, 128, 128])
    b_s = din("gm_b_s", [1, 1024])
    mu = din("rw_mu", [1, 4224])
    w0 = din("rw_w0", [1, D])
    dup = din("rw_decay_up", [64, D])
    a0 = din("rw_a0", [1, D])
    iup = din("rw_iclr_up", [64, D])
    k_k = din("rw_k_k", [1, D])
    k_a = din("rw_k_a", [1, D])
    r_k = din("rw_r_k", [1, D])
    gn_g = din("rw_gn_g", [1, D])
    gn_b = din("rw_gn_b", [1, D])
    w_a = din("w_branch_a", [D, D])
    w_b = din("w_branch_b", [D, D])
    w_o = din("w_out", [D, D])
    g_post = din("norm_post_g", [1, D])
    y = nc.dram_tensor("y", [T, D], F32, kind="ExternalOutput").ap()

    _n = [0]
    stacks = [ExitStack()]

    _bytes = [[0]]
    _peak = [0]

    def sb(shape, dt, name=None):
        _n[0] += 1
        n = 1
        for d_ in shape[1:]:
            n *= d_
        n *= 2 if dt == BF16 else 4
        _bytes[-1][0] += (n + 31) // 32 * 32
        tot = sum(b_[0] for b_ in _bytes)
        _peak[0] = max(_peak[0], tot)
        if os.environ.get("SBSTAT"):
            print(f"  sb {name} {shape} -> total {tot}")
        return stacks[-1].enter_context(nc.sbuf_tensor(f"{name or 'sb'}_{_n[0]}", shape, dt))

    def push():
        stacks.append(ExitStack())
        _bytes.append([0])

    def pop():
        S.barrier()
        stacks.pop().close()
        _bytes.pop()

    def R(name):
        return Region(name)

    PS = nc.alloc_psum_tensor("ps_all", [128, 4096], F32)
    PB = [Region(f"bank{b}", excl=True) for b in range(8)]

    def bank(b, lo=0, hi=512):
        return PS[:, b * 512 + lo: b * 512 + hi]

    brot = [0]
    bankset = [list(range(8))]

    def nextbank(k=1):
        if k == 1 and len(bankset[0]) < 8:
            brot[0] = (brot[0] + 1) % len(bankset[0])
            return bankset[0][brot[0]]
        b = brot[0]
        if b % k:
            b += k - (b % k)
        b %= 8
        brot[0] = (b + k) % 8
        return b

    def mm(out, lhsT, rhs, start=True, stop=True, reads=(), writes=(), sig=True, skip=False):
        return S.op("pe", lambda: nc.tensor.matmul(out, lhsT, rhs, start=start, stop=stop, skip_group_check=skip),
                    reads, writes, sig=sig)

    def act(out, in_, func, reads=(), writes=(), bias=None, scale=None):
        kw = {}
        if bias is not None:
            kw["bias"] = bias
        if scale is not None:
            kw["scale"] = scale
        return S.op("act", lambda: nc.scalar.activation(out=out, in_=in_, func=func, **kw), reads, writes)

    def tt(e, out, in0, in1, op, reads=(), writes=()):
        eng = nc.vector if e == "dve" else nc.gpsimd
        return S.op(e, lambda: eng.tensor_tensor(out=out, in0=in0, in1=in1, op=op), reads, writes)

    def ts(e, out, in0, s1, s2, op0, op1=None, reads=(), writes=()):
        eng = nc.vector if e == "dve" else nc.gpsimd
        if op1 is None:
            return S.op(e, lambda: eng.tensor_scalar(out=out, in0=in0, scalar1=s1, scalar2=None, op0=op0), reads, writes)
        return S.op(e, lambda: eng.tensor_scalar(out=out, in0=in0, scalar1=s1, scalar2=s2, op0=op0, op1=op1), reads, writes)

    def stt(out, in0, scalar, in1, op0, op1, reads=(), writes=()):
        return S.op("dve", lambda: nc.vector.scalar_tensor_tensor(out=out, in0=in0, scalar=scalar, in1=in1, op0=op0, op1=op1), reads, writes)

    def memset(e, ap, val, writes=()):
        eng = nc.vector if e == "dve" else nc.gpsimd
        return S.op(e, lambda: eng.memset(ap, val), (), writes)

    ident_f = sb([128, 128], F32, "ident_f"); r_identf = R("identf")
    ident_b = sb([128, 128], BF16, "ident_b"); r_identb = R("identb")
    I2 = sb([128, 64], BF16, "I2"); r_I2 = R("I2")
    blk1 = sb([128, 128], F32, "blk1"); r_blk1 = R("blk1")
    blk64 = sb([128, 128], F32, "blk64"); r_blk64 = R("blk64")
    ones_row = sb([1, 128], F32, "ones_row"); r_ones = R("ones")
    MASK = sb([128, 320], F32, "MASK"); r_mask = R("mask")
    mask2 = sb([128, 128], F32, "mask2"); r_mask2 = R("mask2")
    segm = sb([128, 512], F32, "segm"); r_segm = R("segm")
    eps_rms = sb([128, 1], F32, "eps_rms"); eps_ln = sb([128, 1], F32, "eps_ln"); eps_gn = sb([128, 1], F32, "eps_gn")
    r_eps = R("eps")

    memset("pool", ident_f[:], 0.0, [r_identf])
    S.op("pool", lambda: nc.gpsimd.affine_select(out=ident_f[:], in_=ident_f[:], pattern=[[-1, 128]],
                                                  compare_op=ALU.not_equal, fill=1.0, base=0, channel_multiplier=1),
         [r_identf], [r_identf])
    S.op("dve", lambda: nc.vector.tensor_copy(out=ident_b[:], in_=ident_f[:]), [r_identf], [r_identb])
    tt("dve", I2[:], ident_f[:, 0:64], ident_f[:, 64:128], ALU.add, [r_identf], [r_I2])
    memset("dve", blk1[:], 0.0, [r_blk1])
    memset("dve", blk1[0:64, 0:64], 1.0, [r_blk1])
    memset("dve", blk1[64:128, 64:128], 1.0, [r_blk1])
    memset("dve", blk64[:], 0.0, [r_blk64])
    memset("dve", blk64[0:64, 0:64], 1.0 / 64, [r_blk64])
    memset("dve", blk64[64:128, 64:128], 1.0 / 64, [r_blk64])
    memset("dve", ones_row[:], 1.0, [r_ones])
    memset("dve", mask2[:], 1.0, [r_mask2])
    memset("dve", mask2[64:128, 0:64], 0.0, [r_mask2])
    memset("dve", segm[:], 1.0, [r_segm])
    memset("dve", segm[:].rearrange("p (c t) -> p c t", t=64)[:, :, 0:1], 0.0, [r_segm])
    memset("dve", eps_rms[:], 1e-6, [r_eps])
    memset("dve", eps_ln[:], 1e-5, [r_eps])
    memset("dve", eps_gn[:], 64e-5, [r_eps])
    memset("pool", MASK[:], 1.0, [r_mask])
    for c0, strict in ((0, True), (64, False), (128, True), (192, False)):
        S.op("pool", lambda c0=c0, strict=strict: nc.gpsimd.affine_select(
            out=MASK[0:64, c0:c0 + 64], in_=MASK[0:64, c0:c0 + 64], pattern=[[1, 64]],
            compare_op=ALU.is_ge, fill=0.0, base=(-1 if strict else 0), channel_multiplier=-1),
            [r_mask], [r_mask])
    S.op("pool", lambda: nc.gpsimd.affine_select(
        out=MASK[0:64, 256:320], in_=MASK[0:64, 256:320], pattern=[[-1, 64]],
        compare_op=ALU.is_ge, fill=0.0, base=-1, channel_multiplier=1), [r_mask], [r_mask])
    S.dma("sp", MASK[64:128, :], MASK[0:64, :], [r_mask], [r_mask])

    chk('const')
    PARAM = sb([128, 128], F32, "PARAM"); r_param = R("param")
    PT = sb([128, 128], F32, "PT"); r_pt = R("pt")
    OM = sb([128, 128], F32, "OM")
    memset("dve", PARAM[:], 0.0, [r_param])
    rows = [(mu, 0, 33), (w0, 33, 8), (a0, 41, 8), (k_k, 49, 8), (k_a, 57, 8), (r_k, 65, 8),
            (gn_g, 73, 8), (gn_b, 81, 8), (ln_g, 89, 8)]
    for ap_, r0, n in rows:
        S.dma("sp", PARAM[r0:r0 + n, :], ap_.rearrange("o (r c) -> (o r) c", c=128), [], [r_param])
    b0 = nextbank()
    mm(bank(b0, 0, 128), PARAM[:], ident_f[:], reads=[r_param, r_identf], writes=[PB[b0]])
    S.op("dve", lambda: nc.vector.tensor_copy(out=PT[:], in_=bank(b0, 0, 128)), [PB[b0]], [r_pt])
    ts("dve", OM[:], PT[:], -1.0, 1.0, ALU.mult, ALU.add, [r_pt], [r_pt])
    C_MU, C_W0, C_A0, C_KK, C_KA, C_RK, C_GG, C_GB, C_LG = 0, 33, 41, 49, 57, 65, 73, 81, 89

    def pcol(c):
        return PT[:, c:c + 1]

    def ocol(c):
        return OM[:, c:c + 1]

    r_gbc = R("gbc")
    r_row = R("row")

    def bcast_row(src_ap, dst, r_dst):
        push()
        ROW = sb([1, 1024], F32, "ROW")
        S.dma("sp", ROW[:], src_ap, [], [r_row])
        for h in range(2):
            b = nextbank()
            mm(bank(b), ones_row[0:1, 0:128], ROW[0:1, h * 512:(h + 1) * 512], reads=[r_ones, r_row], writes=[PB[b]])
            S.op("dve", lambda b=b, h=h: nc.vector.tensor_copy(out=dst[:, h * 512:(h + 1) * 512], in_=bank(b)), [PB[b]], [r_dst])
        pop()

    chk('param')
    hT = sb([128, 8, T], BF16, "hT")
    r_hT = [R(f"hT{i}") for i in range(16)]
    STAT = [sb([128, 16], F32, f"STAT{i}") for i in range(4)]; r_STAT = [R(f"stat{i}") for i in range(4)]
    NWB_P = 2
    WBUF = [sb([128, 8, 128], BF16, f"WB{i}") for i in range(NWB_P)]
    r_WB = [R(f"wb{i}") for i in range(NWB_P)]

    def more_wbufs(n):
        for i in range(n):
            WBUF.append(sb([128, 8, 128], BF16, f"WBx{len(WBUF)}"))
            r_WB.append(R(f"wbx{len(r_WB)}"))

    def drop_wbufs():
        del WBUF[NWB_P:]
        del r_WB[NWB_P:]
        wrot[0] = 0
    VN = sb([128, 16, 1024], BF16, "VN")
    r_VN = [R(f"vn{i}") for i in range(16)]
    YA = sb([128, 8, T], BF16, "YA")
    r_YA = [[R(f"ya{j}_{q}") for q in range(4)] for j in range(8)]

    push()
    WMT = sb([128, 8, 128], BF16, "WMT")
    BIAS = sb([128, 8, 128], F32, "BIAS")
    r_wmt = [R(f"wmt{g}") for g in range(8)]
    WV = sb([128, 8, 1024], BF16, "WV"); r_wv = R("wv")
    for kc in range(8):
        S.dma("pool", WV[:, kc, :], w_in[kc * 128:(kc + 1) * 128, 1024:2048], [], [r_wv])
    push()
    GBC = sb([128, 1024], F32, "GBC")
    bcast_row(g_pre, GBC, r_gbc)
    LNB = sb([128, 1024], F32, "LNB"); r_lnb = R("lnb")
    BSR = sb([1, 1024], F32, "BSR"); r_bsr = R("bsr")
    WS32 = sb([128, 128], F32, "WS32"); r_ws32 = R("ws32")
    WMT32 = sb([128, 8, 128], F32, "WMT32")
    bcast_row(ln_b, LNB, r_lnb)
    S.dma("sp", BSR[:], b_s, [], [r_bsr])
    for g in range(8):
        S.dma("sp", WS32[:], w_s[g], [], [r_ws32])
        b = nextbank()
        mm(bank(b, 0, 128), WS32[:], ident_f[:], reads=[r_ws32, r_identf], writes=[PB[b]])
        tt("dve", WMT32[:, g, :], bank(b, 0, 128), mask2[:], ALU.mult, [PB[b], r_mask2], [r_wmt[g]])
        act(WMT[:, g, :], WMT32[:, g, :], AF.Copy, [r_wmt[g]], [r_wmt[g]])
        b2 = nextbank()
        mm(bank(b2, 0, 128), LNB[:, g * 128:(g + 1) * 128], WMT32[:, g, :], start=True, stop=False,
           reads=[r_lnb, r_wmt[g]], writes=[PB[b2]])
        mm(bank(b2, 0, 128), ones_row[0:1, 0:128], BSR[0:1, g * 128:(g + 1) * 128], start=False, stop=True,
           reads=[r_ones, r_bsr], writes=[PB[b2]])
        S.op("dve", lambda g=g, b2=b2: nc.vector.tensor_copy(out=BIAS[:, g, :], in_=bank(b2, 0, 128)), [PB[b2]], [r_wmt[g]])

    XT = [sb([128, D], F32, f"XT{i}") for i in range(4)]; r_XT = [R(f"xt{i}") for i in range(4)]
    XS = [sb([128, D], BF16, f"XS{i}") for i in range(2)]; r_XS = [R("xs0"), R("xs1")]

    def rstd_of(var_ap, eps_t, out_ap, tmp_ap, reads, rgn):
        act(tmp_ap, var_ap, AF.Sqrt, reads + [r_eps], [rgn], bias=eps_t[:], scale=1.0)
        S.op("dve", lambda: nc.vector.reciprocal(out=out_ap, in_=tmp_ap), [rgn], [rgn])

    A_bank = {}

    def A_s0(i):
        S.dma("sp", XT[i % 4][:], x[i * 128:(i + 1) * 128, :], [], [r_XT[i % 4]])

    def A_s1(i):
        k, k4 = i % 4, i % 4
        st = STAT[k4]
        S.op("dve", lambda: nc.vector.bn_stats(out=st[:, 0:6], in_=XT[k][:, 0:512]), [r_XT[k]], [r_STAT[k4]])
        S.op("dve", lambda: nc.vector.bn_stats(out=st[:, 6:12], in_=XT[k][:, 512:1024]), [r_XT[k]], [r_STAT[k4]])
        S.op("dve", lambda: nc.vector.bn_aggr(out=st[:, 12:14], in_=st[:, 0:12]), [r_STAT[k4]], [r_STAT[k4]])
        stt(st[:, 14:15], st[:, 12:13], st[:, 12:13], st[:, 13:14], ALU.mult, ALU.add, [r_STAT[k4]], [r_STAT[k4]])
        act(st[:, 14:15], st[:, 14:15], AF.Sqrt, [r_STAT[k4], r_eps], [r_STAT[k4]], bias=eps_rms[:], scale=1.0)

    def A_s2(i):
        k, k4, k2 = i % 4, i % 4, i % 2
        st = STAT[k4]
        S.op("dve", lambda: nc.vector.reciprocal(out=st[:, 15:16], in_=st[:, 14:15]), [r_STAT[k4]], [r_STAT[k4]])
        stt(XS[k2][:], XT[k][:], st[:, 15:16], GBC[:], ALU.mult, ALU.mult, [r_XT[k], r_STAT[k4], r_gbc], [r_XS[k2]])
        b = nextbank(2)
        A_bank[i] = b
        for j in range(8):
            bb = b + (j // 4)
            mm(bank(bb, (j % 4) * 128, (j % 4) * 128 + 128), XS[k2][:, j * 128:(j + 1) * 128], ident_b[:],
               reads=[r_XS[k2], r_identb], writes=[PB[bb]], sig=(j % 4 == 3))

    def A_s3(i):
        b = A_bank[i]
        for h in range(2):
            act(hT[:, 4 * h:4 * h + 4, i * 128:(i + 1) * 128],
                bank(b + h).rearrange("p (j t) -> p j t", t=128), AF.Copy, [PB[b + h]], [r_hT[i]])

    for step in range(16 + 3):
        if step < 16:
            A_s0(step)
        if 0 <= step - 1 < 16:
            A_s1(step - 1)
        if 0 <= step - 2 < 16:
            A_s2(step - 2)
        if 0 <= step - 3 < 16:
            A_s3(step - 3)

    chk('A')
    pop()

    def hT_regions(q):
        return r_hT[4 * q:4 * q + 4]

    wrot = [0]

    def load_w(src, c0):
        i = wrot[0]
        wrot[0] = (i + 1) % len(WBUF)
        S.dma("pool", WBUF[i][:], src[:, c0:c0 + 128].rearrange("(kc p) c -> p kc c", p=128), [], [r_WB[i]])
        return i

    def proj_quad(wi, q, b, rhs_src=None, rhs_regions=None):
        for kc in range(8):
            if rhs_src is None:
                rhs = hT[:, kc, q * 512:(q + 1) * 512]
                rr = hT_regions(q)
            else:
                rhs = rhs_src[:, kc, q * 512:(q + 1) * 512]
                rr = rhs_regions(q)
            mm(bank(b), WBUF[wi][:, kc, :], rhs, start=(kc == 0), stop=(kc == 7),
               reads=[r_WB[wi]] + rr, writes=[PB[b]], sig=(kc == 7))

    push()
    more_wbufs(4)
    VG = [sb([128, 1024], F32, f"VG{i}") for i in range(3)]; r_VG = [R(f"vg{i}") for i in range(3)]
    UG = [sb([128, T], F32, f"UG{i}") for i in range(2)]; r_UG = [R("ug0"), R("ug1")]
    ZS = [sb([128, T], F32, f"ZSa{i}") for i in range(2)]; r_ZS = [R("zsa0"), R("zsa1")]
    TMPS = [sb([128, 512], F32, f"TMPS{i}") for i in range(2)]; r_TMPS = [R("tmps0"), R("tmps1")]
    def B_s1(i):
        k = i % 3
        b = nextbank(2)
        for h in range(2):
            for kc in range(8):
                mm(bank(b + h), hT[:, kc, i * 128:(i + 1) * 128], WV[:, kc, h * 512:(h + 1) * 512],
                   start=(kc == 0), stop=(kc == 7), reads=[r_hT[i], r_wv], writes=[PB[b + h]], sig=(kc == 7))
            act(VG[k][:, h * 512:(h + 1) * 512], bank(b + h), AF.Gelu_apprx_tanh, [PB[b + h]], [r_VG[k]])

    def B_s2(i):
        k, k4 = i % 3, i % 4
        st = STAT[k4]
        S.op("dve", lambda: nc.vector.bn_stats(out=st[:, 0:6], in_=VG[k][:, 0:512]), [r_VG[k]], [r_STAT[k4]])
        S.op("dve", lambda: nc.vector.bn_stats(out=st[:, 6:12], in_=VG[k][:, 512:1024]), [r_VG[k]], [r_STAT[k4]])
        S.op("dve", lambda: nc.vector.bn_aggr(out=st[:, 12:14], in_=st[:, 0:12]), [r_STAT[k4]], [r_STAT[k4]])
        act(st[:, 14:15], st[:, 13:14], AF.Sqrt, [r_STAT[k4], r_eps], [r_STAT[k4]], bias=eps_ln[:], scale=1.0)

    def B_s3(i):
        k, k4 = i % 3, i % 4
        st = STAT[k4]
        S.op("dve", lambda: nc.vector.reciprocal(out=st[:, 15:16], in_=st[:, 14:15]), [r_STAT[k4]], [r_STAT[k4]])
        stt(st[:, 14:15], st[:, 12:13], -1.0, st[:, 15:16], ALU.mult, ALU.mult, [r_STAT[k4]], [r_STAT[k4]])
        ts("dve", VN[:, i, :], VG[k][:], st[:, 15:16], st[:, 14:15], ALU.mult, ALU.add, [r_VG[k], r_STAT[k4]], [r_VN[i]])

    for step in range(16 + 2):
        if step < 16:
            B_s1(step)
        if 0 <= step - 1 < 16:
            B_s2(step - 1)
        if 0 <= step - 2 < 16:
            B_s3(step - 2)

    chk('B1')
    wnext = (load_w(w_in, 0), load_w(w_in, 2048))
    for j in range(8):
        k = j % 2
        wu, wz = wnext
        if j < 7:
            wnext = (load_w(w_in, (j + 1) * 128), load_w(w_in, 2048 + (j + 1) * 128))
        for q in range(4):
            b = nextbank()
            proj_quad(wu, q, b)
            act(UG[k][:, q * 512:(q + 1) * 512], bank(b), AF.Gelu_apprx_tanh, [PB[b]], [r_UG[k]])
        for q in range(4):
            b = nextbank()
            proj_quad(wz, q, b)
            act(ZS[k][:, q * 512:(q + 1) * 512], bank(b), AF.Silu, [PB[b]], [r_ZS[k]])
        tt("pool", UG[k][:], UG[k][:], ZS[k][:], ALU.mult, [r_UG[k], r_ZS[k]], [r_UG[k]])
        for q in range(4):
            b = nextbank()
            for ii in range(4):
                i = 4 * q + ii
                mm(bank(b, ii * 128, ii * 128 + 128), VN[:, i, j * 128:(j + 1) * 128], WMT[:, j, :],
                   reads=[r_VN[i], r_wmt[j]], writes=[PB[b]], sig=(ii == 3))
            kk_ = q % 2
            stt(TMPS[kk_][:].rearrange("p (a t) -> p a t", t=128), bank(b).rearrange("p (a t) -> p a t", t=128),
                pcol(C_LG + j), BIAS[:, j:j + 1, :].to_broadcast([128, 4, 128]), ALU.mult, ALU.add,
                [PB[b], r_pt, r_wmt[j]], [r_TMPS[kk_]])
            tt("dve", YA[:, j, q * 512:(q + 1) * 512], TMPS[kk_][:], UG[k][:, q * 512:(q + 1) * 512], ALU.mult,
               [r_TMPS[kk_], r_UG[k]], [r_YA[j][q]])

    chk('B2')
    drop_wbufs()
    pop()
    pop()
    push()
    YB = VN
    YBv = VN[:].rearrange("p a b -> p (a b)").rearrange("p (j t) -> p j t", t=T)
    r_YB = [[R(f"yb{j}_{q}") for q in range(4)] for j in range(8)]
    r_VN_all = r_VN

    TA = sb([128, T], BF16, "TA"); r_TA = R("ta")
    DUP = sb([128, D], BF16, "DUP"); r_dup = R("dup")
    S.dma("pool", DUP[0:64, :], dup, [], [r_dup])
    S.dma("pool", DUP[64:128, :], iup, [], [r_dup])
    push()
    SHB = sb([128, 513], F32, "SHB"); r_SHB = R("shb")
    CAR = sb([128, 8], F32, "CAR")
    XL = sb([128, 512], F32, "XL"); r_XL = R("xl")

    def shift_evac(b, n, kd, OUT, r_OUT, first):
        if first:
            memset("pool", SHB[:, 0:1], 0.0, [r_SHB])
        else:
            S.op("pool", lambda: nc.gpsimd.tensor_copy(out=SHB[:, 0:1], in_=CAR[:, kd:kd + 1]), [r_SHB], [r_SHB])
        act(SHB[:, 1:513], bank(b), AF.Identity, [PB[b], r_pt, r_SHB], [r_SHB], scale=pcol(C_MU + n))
        stt(OUT, bank(b), ocol(C_MU + n), SHB[:, 0:512], ALU.mult, ALU.add, [PB[b], r_pt, r_SHB], [r_OUT])
        S.op("pool", lambda: nc.gpsimd.tensor_copy(out=CAR[:, kd:kd + 1], in_=SHB[:, 512:513]), [r_SHB], [r_SHB])

    wl = load_w(w_in, PB0 + 4096)
    for q in range(4):
        b = nextbank()
        proj_quad(wl, q, b)
        shift_evac(b, 32, 4, XL[:], r_XL, q == 0)
        act(TA[0:64, q * 512:(q + 1) * 512], XL[0:64, :], AF.Tanh, [r_XL], [r_TA])
        act(TA[64:128, q * 512:(q + 1) * 512], XL[64:128, :], AF.Copy, [r_XL], [r_TA])

    chk('C0')
    pop()
    W4 = [sb([128, 8, 128], BF16, f"W4_{kd}") for kd in range(4)]
    r_W4 = [R(f"w4_{kd}") for kd in range(4)]

    def scr(name, dt=F32, w=512, n=1):
        return [sb([128, w], dt, f"{name}{s}") for s in range(n)], [R(f"{name}{s}") for s in range(n)]

    Xr, r_Xr = scr("Xr"); Xk, r_Xk = scr("Xk"); Xv, r_Xv = scr("Xv")
    SG, r_SG = scr("SG"); AAt, r_AA = scr("AA"); CC, r_CC = scr("CC")
    Wi, r_Wi = scr("Wi", BF16); Wx, r_Wx = scr("Wx", BF16)
    KQ, r_KQ = scr("KQ")
    CX, r_CX = SG, r_SG
    BH, r_BH = SG, r_SG
    KKN, r_KKN = CC, r_CC
    RS, r_RS = KQ, r_KQ
    DD, r_DD = scr("DD"); DSQ, r_DSQ = scr("DSQ")
    TA_, r_TA_ = DD, r_DD
    KM, r_KM = DSQ, r_DSQ
    RK, r_RK = DD, r_DD
    SH = [sb([128, 513], F32, f"SH{kd}") for kd in range(4)]; r_SH = [R(f"sh{kd}") for kd in range(4)]
    Xz, r_Xz = scr("Xz", n=2); Wt, r_Wt = scr("Wt", n=2); BON, r_BON = scr("BON", n=2)
    AR, r_AR = scr("AR", BF16, 1024, n=2)
    BBt, r_BB = scr("BBt", BF16, n=2); KKt, r_KK = scr("KKt", BF16, n=2)
    BBAR, r_BBAR = scr("BBAR", BF16, n=2); KBAR, r_KBAR = scr("KBAR", BF16, n=2); VB, r_VB = scr("VB", BF16, n=2)
    OF, r_OF = scr("OF", n=2)
    NG = 4
    CH = [sb([128, NG, 320], BF16, f"CH{g}") for g in range(2)]; r_CH = [R(f"ch{g}") for g in range(2)]
    GM = [sb([128, NG, 320], BF16, f"GM{g}") for g in range(2)]; r_GM = [R(f"gm{g}") for g in range(2)]
    RB = [[sb([128, NG, 192], BF16, f"RB{g}_{s}") for s in range(2)] for g in range(2)]
    r_RB = [[R(f"rb{g}_{s}") for s in range(2)] for g in range(2)]
    TTF = [sb([128, NG, 64], BF16, f"TTF{g}") for g in range(2)]; r_TTF = [R(f"ttf{g}") for g in range(2)]
    US = [sb([128, NG, 128], BF16, f"US{g}") for g in range(2)]; r_US = [R(f"us{g}") for g in range(2)]
    PO = [sb([128, NG, 128], BF16, f"PO{g}") for g in range(2)]; r_PO = [R(f"po{g}") for g in range(2)]
    QQ = [sb([128, NG, 64], F32, f"QQ{g}") for g in range(2)]; r_QQ = [R(f"qq{g}") for g in range(2)]
    SWQ = [sb([128, 64], F32, f"SWQ{i}") for i in range(2)]; r_SWQ = [R(f"swq{i}") for i in range(2)]
    SQ = sb([128, 9, 64], BF16, "SQ"); r_SQ = [R(f"sq{i}") for i in range(9)]

    HALF = ((0, 64), (64, 128))
    NCH = [Region(f"nch{g}", excl=True) for g in range(2)]
    NOO = [Region(f"noo{g}", excl=True) for g in range(2)]
    bankset[0] = [6, 7]

    def load_pair_weights(p):
        for kd in range(4):
            S.dma("pool", W4[kd][:], w_in[:, PB0 + kd * 1024 + p * 128: PB0 + kd * 1024 + (p + 1) * 128]
                  .rearrange("(kc p) c -> p kc c", p=128), [], [r_W4[kd]])

    def v3(ap):
        return ap.rearrange("p (c t) -> p c t", t=64)

    iters = [(p, q) for p in range(8) for q in range(4)]

    def C1_gen(n):
        p, q = iters[n]
        s = n % 2
        B6, B7 = 6, 7
        if q == 0:
            load_pair_weights(p)

        def proj(kd, b):
            for kc in range(8):
                mm(bank(b), W4[kd][:, kc, :], hT[:, kc, q * 512:(q + 1) * 512], start=(kc == 0), stop=(kc == 7),
                   reads=[r_W4[kd]] + hT_regions(q), writes=[PB[b]], sig=(kc == 7))

        def sh_act(kd, b):
            act(SH[kd][:, 1:513], bank(b), AF.Identity, [PB[b], r_pt, r_SH[kd]], [r_SH[kd]], scale=pcol(C_MU + kd * 8 + p))

        def sh_dve(kd, b, OUT, r_OUT):
            stt(OUT, bank(b), ocol(C_MU + kd * 8 + p), SH[kd][:, 0:512], ALU.mult, ALU.add, [PB[b], r_pt, r_SH[kd]], [r_OUT])

        def sh_carry(kd):
            S.op("pool", lambda: nc.gpsimd.tensor_copy(out=SH[kd][:, 0:1], in_=SH[kd][:, 512:513]), [r_SH[kd]], [r_SH[kd]])

        ARv = AR[s][:].rearrange("p (c two t) -> p c two t", two=2, t=64)
        wcb = v3(Wt[s][:])[:, :, 63:64].to_broadcast([128, 8, 64])
        if q == 0:
            for kd in range(4):
                memset("pool", SH[kd][:, 0:1], 0.0, [r_SH[kd]])
        proj(0, B6); proj(1, B7)
        yield
        sh_act(0, B6); sh_act(1, B7)
        yield
        sh_dve(0, B6, Xr[0][:], r_Xr[0]); sh_dve(1, B7, Xk[0][:], r_Xk[0])
        yield "cut"
        proj(2, B6); proj(3, B7); sh_carry(0); sh_carry(1)
        act(KQ[0][:], Xk[0][:], AF.Square, [r_Xk[0], r_pt], [r_KQ[0]], scale=pcol(C_KK + p))
        yield
        sh_act(2, B6); sh_act(3, B7)
        yield
        sh_dve(2, B6, Xv[0][:], r_Xv[0]); sh_dve(3, B7, Xz[s][:], r_Xz[s])
        yield "cut"
        mm(bank(B6), DUP[0:64, p * 128:(p + 1) * 128], TA[0:64, q * 512:(q + 1) * 512], reads=[r_dup, r_TA], writes=[PB[B6]])
        mm(bank(B7), DUP[64:128, p * 128:(p + 1) * 128], TA[64:128, q * 512:(q + 1) * 512], reads=[r_dup, r_TA], writes=[PB[B7]])
        sh_carry(2); sh_carry(3)
        yield
        act(SG[0][:], bank(B6), AF.Sigmoid, [PB[B6], r_pt], [r_SG[0]], bias=pcol(C_W0 + p), scale=1.0)
        act(AAt[0][:], bank(B7), AF.Sigmoid, [PB[B7], r_pt], [r_AA[0]], bias=pcol(C_A0 + p), scale=1.0)
        S.op("pool", lambda: nc.gpsimd.tensor_copy(out=VB[s][:], in_=Xv[0][:]), [r_Xv[0]], [r_VB[s]])
        yield "cut"
        mm(bank(B6), blk1[:], KQ[0][:], reads=[r_blk1, r_KQ[0]], writes=[PB[B6]])
        S.op("dve", lambda: nc.vector.tensor_tensor_scan(out=CC[0][:], data0=segm[:], data1=SG[0][:], initial=0.0,
                                                         op0=ALU.mult, op1=ALU.add), [r_segm, r_SG[0]], [r_CC[0]])
        act(Xz[s][:], Xz[s][:], AF.Silu, [r_Xz[s]], [r_Xz[s]])
        ts("dve", TA_[0][:], AAt[0][:], pcol(C_KA + p), ocol(C_KA + p), ALU.mult, ALU.add, [r_AA[0], r_pt], [r_TA_[0]])
        yield
        ts("dve", RS[0][:], bank(B6), 1e-12, None, ALU.max, None, [PB[B6]], [r_RS[0]])
        act(Wt[s][:], CC[0][:], AF.Exp, [r_CC[0]], [r_Wt[s]], scale=-DSCALE)
        act(Wi[0][:], CC[0][:], AF.Exp, [r_CC[0]], [r_Wi[0]], scale=DSCALE)
        tt("pool", CX[0][:], CC[0][:], SG[0][:], ALU.subtract, [r_CC[0], r_SG[0]], [r_CX[0]])
        tt("pool", KM[0][:], Xk[0][:], TA_[0][:], ALU.mult, [r_Xk[0], r_TA_[0]], [r_KM[0]])
        yield
        act(Wx[0][:], CX[0][:], AF.Exp, [r_CX[0]], [r_Wx[0]], scale=-DSCALE)
        act(RS[0][:], RS[0][:], AF.Ln, [r_RS[0]], [r_RS[0]])
        tt("pool", ARv[:, :, 1, :], v3(Xr[0][:]), v3(Wt[s][:]), ALU.mult, [r_Xr[0], r_Wt[s]], [r_AR[s]])
        tt("pool", KKt[s][:], KM[0][:], Wi[0][:], ALU.mult, [r_KM[0], r_Wi[0]], [r_KK[s]])
        stt(RK[0][:], Xr[0][:], pcol(C_RK + p), KM[0][:], ALU.mult, ALU.mult, [r_Xr[0], r_pt, r_KM[0]], [r_RK[0]])
        yield "cut"
        mm(bank(B7), blk1[:], RK[0][:], reads=[r_blk1, r_RK[0]], writes=[PB[B7]])
        act(RS[0][:], RS[0][:], AF.Exp, [r_RS[0]], [r_RS[0]], scale=-0.5)
        tt("pool", v3(KBAR[s][:]), v3(KKt[s][:]), wcb, ALU.mult, [r_KK[s], r_Wt[s]], [r_KBAR[s]])
        yield
        tt("dve", BON[s][:], bank(B7), Xv[0][:], ALU.mult, [PB[B7], r_Xv[0]], [r_BON[s]])
        stt(KKN[0][:], Xk[0][:], pcol(C_KK + p), RS[0][:], ALU.mult, ALU.mult, [r_Xk[0], r_pt, r_RS[0]], [r_KKN[0]])
        yield
        stt(ARv[:, :, 0, :], v3(KKN[0][:]), -1.0, v3(Wx[0][:]), ALU.mult, ALU.mult, [r_KKN[0], r_Wx[0]], [r_AR[s]])
        tt("pool", BH[0][:], KKN[0][:], AAt[0][:], ALU.mult, [r_KKN[0], r_AA[0]], [r_BH[0]])
        yield
        tt("pool", BBt[s][:], BH[0][:], Wi[0][:], ALU.mult, [r_BH[0], r_Wi[0]], [r_BB[s]])
        yield
        tt("pool", v3(BBAR[s][:]), v3(BBt[s][:]), wcb, ALU.mult, [r_BB[s], r_Wt[s]], [r_BBAR[s]])
        yield

    def C3_gen(n):
        p, q = iters[n]
        s = n % 2
        b1, b2 = 6, 7
        mm(bank(b1), blk64[:], OF[s][:], reads=[r_blk64, r_OF[s]], writes=[PB[b1]])
        yield
        stt(DD[0][:], bank(b1), -1.0, OF[s][:], ALU.mult, ALU.add, [r_OF[s], PB[b1]], [r_DD[0]])
        yield
        act(DSQ[0][:], DD[0][:], AF.Square, [r_DD[0]], [r_DSQ[0]])
        yield "cut"
        mm(bank(b2), blk64[:], DSQ[0][:], reads=[r_blk64, r_DSQ[0]], writes=[PB[b2]])
        yield
        act(DSQ[0][:], bank(b2), AF.Ln, [PB[b2], r_eps], [r_DSQ[0]], bias=eps_gn[:], scale=1.0)
        yield
        act(DSQ[0][:], DSQ[0][:], AF.Exp, [r_DSQ[0]], [r_DSQ[0]], scale=-0.5)
        yield
        stt(DD[0][:], DD[0][:], pcol(C_GG + p), DSQ[0][:], ALU.mult, ALU.mult, [r_DD[0], r_pt, r_DSQ[0]], [r_DD[0]])
        yield
        stt(DD[0][:], DD[0][:], pcol(C_GB + p), BON[s][:], ALU.add, ALU.add, [r_DD[0], r_pt, r_BON[s]], [r_DD[0]])
        yield
        tt("dve", YBv[:, p, q * 512:(q + 1) * 512], DD[0][:], Xz[s][:], ALU.mult, [r_DD[0], r_Xz[s]],
           [r_YB[p][q]] + r_VN_all)
        yield "cut"

    def run_bg(gen, k):
        for _ in range(k):
            try:
                if next(gen) == "cut":
                    return
            except StopIteration:
                return

    def chain_gens(gens):
        for g_ in gens:
            yield from g_

    def grp(calls):
        los = [c_ for c_ in calls if c_[0].base_partition() == 0]
        his = [c_ for c_ in calls if c_[0].base_partition() != 0]
        if len(los) == len(his) and len(los) > 0:
            calls = [c_ for pair_ in zip(los, his) for c_ in pair_]
        for n_, c_ in enumerate(calls):
            o_, l_, r_, st_, sp_, rd_, wr_ = c_[:7]
            mm(o_, l_, r_, start=st_, stop=sp_, reads=rd_, writes=wr_, sig=(n_ == len(calls) - 1),
               skip=(len(c_) > 7 and c_[7]))

    def C2(n, bg):
        p, q = iters[n]
        s = n % 2
        GRP = (0, 1)
        gb0 = (0, 3)

        def Wc(g, j, c0, c1, lo=0, hi=128):
            base = gb0[g] * 512 + j * 256
            return PS[lo:hi, base + c0: base + c1]

        def Nc(g, j, c0, c1, lo=0, hi=128):
            base = (gb0[g] + 2) * 512 + j * 128
            return PS[lo:hi, base + c0: base + c1]

        def Wall(g, c0, c1):
            return PS[:, gb0[g] * 512: gb0[g] * 512 + 1024].rearrange("p (j w) -> p j w", w=256)[:, :, c0:c1]

        def Nall(g, c0, c1):
            return PS[:, (gb0[g] + 2) * 512: (gb0[g] + 3) * 512].rearrange("p (j w) -> p j w", w=128)[:, :, c0:c1]

        def rW(g):
            return [PB[gb0[g]], PB[gb0[g] + 1]]

        def rN(g):
            return [NCH[g], NOO[g]]

        def wb(g, j):
            return [PB[gb0[g] + j // 2]]

        def ck(g, j):
            return g * NG + j

        def cp(g, out, in_, reads, writes):
            if g == 0:
                act(out, in_, AF.Copy, reads, writes)
            else:
                S.op("dve", lambda: nc.vector.tensor_copy(out=out, in_=in_), reads, writes)

        def TR_calls(nn, g, j):
            sN = nn % 2
            c = ck(g, j)
            calls = []
            srcs = ((BBAR[sN], r_BBAR[sN], None), (KBAR[sN], r_KBAR[sN], None), (VB[sN], r_VB[sN], None), (AR[sN], r_AR[sN], 0))
            for si, (src, rs_, arsel) in enumerate(srcs):
                for (lo, hi) in HALF:
                    l = src[lo:hi, c * 64:(c + 1) * 64] if arsel is None else src[lo:hi, c * 128:c * 128 + 64]
                    calls.append((Wc(g, j, si * 64, si * 64 + 64, lo, hi), l, ident_b[lo:hi, lo:hi], True, True,
                                  [rs_, r_identb], wb(g, j)))
            return calls

        if n == 0:
            for g in GRP:
                for j in range(NG):
                    grp(TR_calls(0, g, j))
            for g in GRP:
                cp(g, CH[g][:, :, 0:256], Wall(g, 0, 256), rW(g), [r_CH[g]])
        run_bg(bg, 5)
        for g in GRP:
            for j in range(NG):
                c = ck(g, j)
                calls = []
                for (lo, hi) in HALF:
                    arg = AR[s][lo:hi, c * 128:(c + 1) * 128]
                    calls.append((Wc(g, j, 0, 128, lo, hi), BBt[s][lo:hi, c * 64:(c + 1) * 64], arg, True, True,
                                  [r_BB[s], r_AR[s]], wb(g, j)))
                    calls.append((Wc(g, j, 128, 256, lo, hi), KKt[s][lo:hi, c * 64:(c + 1) * 64], arg, True, True,
                                  [r_KK[s], r_AR[s]], wb(g, j)))
                    calls.append((Nc(g, j, 0, 64, lo, hi), AR[s][lo:hi, c * 128:c * 128 + 64], BBt[s][lo:hi, c * 64:(c + 1) * 64],
                                  True, True, [r_BB[s], r_AR[s]], rN(g)))
                grp(calls)
        for g in GRP:
            mk = lambda c0, c1: MASK[:, c0:c1].unsqueeze(1).to_broadcast([128, NG, c1 - c0])
            tt("dve", RB[g][0][:, :, 64:128], Wall(g, 0, 64), mk(0, 64), ALU.mult, rW(g) + [r_mask], [r_RB[g][0]])
            tt("dve", GM[g][:, :, 64:256], Wall(g, 64, 256), mk(64, 256), ALU.mult, rW(g) + [r_mask], [r_GM[g]])
            tt("dve", RB[g][0][:, :, 128:192], Nall(g, 0, 64), mk(256, 320), ALU.mult, rN(g) + [r_mask], [r_RB[g][0]])
        run_bg(bg, 5)
        for g in GRP:
            for j in range(NG):
                calls = []
                for (lo, hi) in HALF:
                    calls.append((Nc(g, j, 64, 128, lo, hi), GM[g][lo:hi, j, 128:192], CH[g][lo:hi, j, 128:192], True, True,
                                  [r_GM[g], r_CH[g]], rN(g)))
                grp(calls)
        for g in GRP:
            cp(g, CH[g][:, :, 256:320], Nall(g, 64, 128), rN(g), [r_CH[g]])
        run_bg(bg, 5)
        for r in range(1, 6):
            cur, nxt = (r - 1) % 2, r % 2
            for g in GRP:
                for j in range(NG):
                    calls = []
                    for (lo, hi) in HALF:
                        Ncur = RB[g][cur][lo:hi, j, 128:192]
                        Zcur = RB[g][cur][lo:hi, j, 64:128]
                        if r == 1:
                            calls.append((Wc(g, j, 0, 64, lo, hi), ident_b[lo:hi, lo:hi], I2[lo:hi, :], True, False,
                                          [r_identb, r_I2], wb(g, j)))
                            calls.append((Wc(g, j, 0, 64, lo, hi), ident_b[lo:hi, lo:hi], Zcur, False, True,
                                          [r_identb, r_RB[g][cur]], wb(g, j)))
                            calls.append((Wc(g, j, 64, 128, lo, hi), Ncur, Zcur, True, True, [r_RB[g][cur]], wb(g, j)))
                        else:
                            calls.append((Wc(g, j, 0, 64, lo, hi), ident_b[lo:hi, lo:hi], RB[g][cur][lo:hi, j, 0:64], True, False,
                                          [r_identb, r_RB[g][cur]], wb(g, j), True))
                            calls.append((Wc(g, j, 0, 128, lo, hi), Ncur, RB[g][cur][lo:hi, j, 0:128], False, True, [r_RB[g][cur]], wb(g, j), True))
                        calls.append((Wc(g, j, 128, 192, lo, hi), Zcur, Ncur, True, True, [r_RB[g][cur]], wb(g, j)))
                    grp(calls)
            for g in GRP:
                cp(g, RB[g][nxt][:, :, 0:192], Wall(g, 0, 192), rW(g), [r_RB[g][nxt]])
            run_bg(bg, 5)
        fin = 5 % 2
        for g in GRP:
            for j in range(NG):
                calls = []
                for (lo, hi) in HALF:
                    calls.append((Wc(g, j, 0, 64, lo, hi), ident_b[lo:hi, lo:hi], RB[g][fin][lo:hi, j, 0:64], True, False,
                                  [r_identb, r_RB[g][fin]], wb(g, j)))
                    calls.append((Wc(g, j, 0, 64, lo, hi), RB[g][fin][lo:hi, j, 128:192], RB[g][fin][lo:hi, j, 0:64], False, True,
                                  [r_RB[g][fin]], wb(g, j)))
                grp(calls)
        for g in GRP:
            cp(g, TTF[g][:], Wall(g, 0, 64), rW(g), [r_TTF[g]])
        run_bg(bg, 5)
        for g in GRP:
            for j in range(NG):
                calls = []
                for (lo, hi) in HALF:
                    calls.append((Wc(g, j, 0, 128, lo, hi), TTF[g][lo:hi, j, :], CH[g][lo:hi, j, 192:320], True, True,
                                  [r_TTF[g], r_CH[g]], wb(g, j)))
                grp(calls)
        for g in GRP:
            cp(g, US[g][:], Wall(g, 0, 128), rW(g), [r_US[g]])
        run_bg(bg, 5)
        for g in GRP:
            for j in range(NG):
                calls = []
                for (lo, hi) in HALF:
                    calls.append((Wc(g, j, 0, 64, lo, hi), US[g][lo:hi, j, 0:64], CH[g][lo:hi, j, 0:64], True, True,
                                  [r_US[g], r_CH[g]], wb(g, j)))
                    calls.append((Wc(g, j, 64, 128, lo, hi), ident_b[lo:hi, lo:hi], AR[s][lo:hi, ck(g, j) * 128 + 64:(ck(g, j) + 1) * 128],
                                  True, False, [r_identb, r_AR[s]], wb(g, j)))
                    calls.append((Wc(g, j, 64, 128, lo, hi), US[g][lo:hi, j, 0:64], GM[g][lo:hi, j, 64:128], False, True,
                                  [r_US[g], r_GM[g]], wb(g, j)))
                for (lo, hi) in HALF:
                    calls.append((Nc(g, j, 0, 64, lo, hi), CH[g][lo:hi, j, 0:64], US[g][lo:hi, j, 64:128], True, False,
                                  [r_CH[g], r_US[g]], rN(g)))
                    calls.append((Nc(g, j, 0, 64, lo, hi), CH[g][lo:hi, j, 64:128], CH[g][lo:hi, j, 128:192], False, True,
                                  [r_CH[g]], rN(g)))
                grp(calls)
        for g in GRP:
            cp(g, PO[g][:], Wall(g, 0, 128), rW(g), [r_PO[g]])
            cp(g, QQ[g][:], Nall(g, 0, 64), rN(g), [r_QQ[g]])
        run_bg(bg, 5)
        for _ in bg:
            pass
        have_next = n + 1 < len(iters)
        if q == 0:
            memset("dve", SQ[:, 0, :], 0.0, [r_SQ[0]])
        else:
            S.op("dve", lambda: nc.vector.tensor_copy(out=SQ[:, 0, :], in_=SQ[:, 8, :]), [r_SQ[8]], [r_SQ[0]])

        def OO_calls(g, j):
            c = ck(g, j)
            calls = []
            for (lo, hi) in HALF:
                o_ = PS[lo:hi, 6 * 512 + c * 64: 6 * 512 + (c + 1) * 64]
                calls.append((o_, US[g][lo:hi, j, 64:128], GM[g][lo:hi, j, 64:128], True, False,
                              [r_US[g], r_GM[g]], [PB[6]]))
                calls.append((o_, CH[g][lo:hi, j, 128:192], GM[g][lo:hi, j, 192:256], False, False,
                              [r_CH[g], r_GM[g]], [PB[6]]))
                calls.append((o_, SQ[lo:hi, c, :], PO[g][lo:hi, j, 64:128], False, True,
                              [r_SQ[c], r_PO[g]], [PB[6]]))
            return calls

        prev = None
        for g in GRP:
            for j in range(NG):
                c = ck(g, j)
                k2 = c % 2
                stt(SWQ[k2][:], SQ[:, c, :], Wt[s][:, c * 64 + 63: c * 64 + 64], QQ[g][:, j, :], ALU.mult, ALU.add,
                    [r_SQ[c], r_Wt[s], r_QQ[g]], [r_SWQ[k2]])
                calls = []
                for (lo, hi) in HALF:
                    calls.append((Nc(g, j, 0, 64, lo, hi), PO[g][lo:hi, j, 0:64], SQ[lo:hi, c, :], True, True,
                                  [r_PO[g], r_SQ[c]], [NCH[g]]))
                grp(calls)
                tt("dve", SQ[:, c + 1, :], Nc(g, j, 0, 64), SWQ[k2][:], ALU.add, [NCH[g], r_SWQ[k2]], [r_SQ[c + 1]])
                if prev is not None:
                    grp(OO_calls(*prev))
                if have_next:
                    grp(TR_calls(n + 1, g, j))
                prev = (g, j)
                if have_next and c == NG:
                    act(CH[0][:, :, 0:256], Wall(0, 0, 256), AF.Copy, rW(0), [r_CH[0]])
        grp(OO_calls(*prev))
        S.op("dve", lambda: nc.vector.tensor_copy(out=OF[s][:], in_=bank(6)), [PB[6]], [r_OF[s]])
        if have_next:
            act(CH[1][:, :, 0:256], Wall(1, 0, 256), AF.Copy, rW(1), [r_CH[1]])

    for _ in C1_gen(0):
        pass
    for n in range(len(iters)):
        gens = []
        if n >= 1:
            gens.append(C3_gen(n - 1))
        if n + 1 < len(iters):
            gens.append(C1_gen(n + 1))
        bg = chain_gens(gens)
        C2(n, bg)
        for _ in bg:
            pass
        if n == 0:
            chk('C2first')
    for _ in C3_gen(len(iters) - 1):
        pass
    bankset[0] = list(range(8))
    chk('C')
    pop()
    push()
    MG = sb([128, 8, T], BF16, "MG")
    r_MG = [[R(f"mg{j}_{q}") for q in range(4)] for j in range(8)]
    push()
    more_wbufs(6)
    GA = [sb([128, 512], F32, f"GA{i}") for i in range(2)]; r_GA = [R("ga0"), R("ga1")]
    GBt = [sb([128, 512], F32, f"GB{i}") for i in range(2)]; r_GB = [R("gb0"), R("gb1")]
    MA = [sb([128, 512], F32, f"MA{i}") for i in range(2)]; r_MA = [R("ma0"), R("ma1")]
    MBt = [sb([128, 512], F32, f"MB{i}") for i in range(2)]; r_MB = [R("mb0"), R("mb1")]

    def ya_regions(q):
        return [r_YA[j][q] for j in range(8)]

    def yb_regions(q):
        return [r_YB[j][q] for j in range(8)]

    wn = (load_w(w_in, PG0), load_w(w_a, 0), load_w(w_in, PG0 + 1024), load_w(w_b, 0))
    for j in range(8):
        wga, wa_, wgb, wb_ = wn
        for q in range(4):
            k = q % 2
            b = nextbank()
            proj_quad(wga, q, b)
            act(GA[k][:], bank(b), AF.Sigmoid, [PB[b]], [r_GA[k]])
            b = nextbank()
            proj_quad(wa_, q, b, YA, ya_regions)
            tt("dve", MA[k][:], bank(b), GA[k][:], ALU.mult, [PB[b], r_GA[k]], [r_MA[k]])
            b = nextbank()
            proj_quad(wgb, q, b)
            act(GBt[k][:], bank(b), AF.Sigmoid, [PB[b]], [r_GB[k]])
            b = nextbank()
            proj_quad(wb_, q, b, YBv, yb_regions)
            tt("dve", MBt[k][:], bank(b), GBt[k][:], ALU.mult, [PB[b], r_GB[k]], [r_MB[k]])
            tt("pool", MG[:, j, q * 512:(q + 1) * 512], MA[k][:], MBt[k][:], ALU.add, [r_MA[k], r_MB[k]], [r_MG[j][q]])
            if q == 1 and j < 7:
                c1 = (j + 1) * 128
                wn = (load_w(w_in, PG0 + c1), load_w(w_a, c1), load_w(w_in, PG0 + 1024 + c1), load_w(w_b, c1))

    chk('D')
    drop_wbufs()
    pop()
    push()
    WO = sb([128, 8, 1024], BF16, "WO"); r_wv = R("wo")
    for kc in range(8):
        S.dma("pool", WO[:, kc, :], w_o[kc * 128:(kc + 1) * 128, :], [], [r_wv])
    OT = [sb([128, D], F32, f"OT{i}") for i in range(3)]; r_OT = [R(f"ot{i}") for i in range(3)]
    XT = [sb([128, D], F32, f"XTe{i}") for i in range(3)]; r_XT = [R(f"xte{i}") for i in range(3)]
    GBC = sb([128, 1024], F32, "GBCe")
    bcast_row(g_post, GBC, r_gbc)
    out_toks = []
    E_bank = {}

    def E_s1(i):
        k, k4 = i % 3, i % 4
        q = i // 4
        S.dma("sp", XT[k][:], x[i * 128:(i + 1) * 128, :], [], [r_XT[k]])
        b = nextbank(2)
        E_bank[i] = b
        for h in range(2):
            for kc in range(8):
                mm(bank(b + h), MG[:, kc, i * 128:(i + 1) * 128], WO[:, kc, h * 512:(h + 1) * 512],
                   start=(kc == 0), stop=(kc == 7), reads=[r_MG[kc][q], r_wv], writes=[PB[b + h]], sig=(kc == 7))
        st = STAT[k4]
        S.op("dve", lambda: nc.vector.bn_stats(out=st[:, 0:6], in_=bank(b)), [PB[b]], [r_STAT[k4]])
        S.op("dve", lambda: nc.vector.bn_stats(out=st[:, 6:12], in_=bank(b + 1)), [PB[b + 1]], [r_STAT[k4]])
        S.op("dve", lambda: nc.vector.bn_aggr(out=st[:, 12:14], in_=st[:, 0:12]), [r_STAT[k4]], [r_STAT[k4]])
        stt(st[:, 14:15], st[:, 12:13], st[:, 12:13], st[:, 13:14], ALU.mult, ALU.add, [r_STAT[k4]], [r_STAT[k4]])
        act(st[:, 14:15], st[:, 14:15], AF.Sqrt, [r_STAT[k4], r_eps], [r_STAT[k4]], bias=eps_rms[:], scale=1.0)

    def E_s2(i):
        k, k4 = i % 3, i % 4
        b = E_bank[i]
        st = STAT[k4]
        S.op("dve", lambda: nc.vector.reciprocal(out=st[:, 15:16], in_=st[:, 14:15]), [r_STAT[k4]], [r_STAT[k4]])
        for h in range(2):
            stt(OT[k][:, h * 512:(h + 1) * 512], bank(b + h), st[:, 15:16], GBC[:, h * 512:(h + 1) * 512], ALU.mult, ALU.mult,
                [PB[b + h], r_STAT[k4], r_gbc], [r_OT[k]])

    def E_s3(i):
        k = i % 3
        tt("pool", OT[k][:], OT[k][:], XT[k][:], ALU.add, [r_OT[k], r_XT[k]], [r_OT[k]])

    def E_s4(i):
        k = i % 3
        out_toks.append(S.dma("sp", y[i * 128:(i + 1) * 128, :], OT[k][:], [r_OT[k]], []))

    for step in range(16 + 3):
        if step < 16:
            E_s1(step)
        if 0 <= step - 1 < 16:
            E_s2(step - 1)
        if 0 <= step - 2 < 16:
            E_s3(step - 2)
        if 0 <= step - 3 < 16:
            E_s4(step - 3)
    for tok in out_toks:
        S.wait_tok("sp", tok)
    pop()
    pop()
    stacks.pop().close()


_CACHE = {}


def kernel(**inputs):
    if "nc" not in _CACHE:
        _CACHE["nc"] = build_program()[0]
    nc = _CACHE["nc"]
    f = lambda a: np.ascontiguousarray(np.asarray(a, dtype=np.float32))
    x = f(inputs["x"])
    common = {
        "norm_pre_g": f(inputs["norm_pre_g"]).reshape(1, D),
        "w_in": f(inputs["w_in"]).reshape(D, NCOL),
        "gm_ln_g": f(inputs["gm_ln_g"]).reshape(1, D),
        "gm_ln_b": f(inputs["gm_ln_b"]).reshape(1, D),
        "gm_w_s": f(inputs["gm_w_s"]).reshape(8, 128, 128),
        "gm_b_s": f(inputs["gm_b_s"]).reshape(1, 1024),
        "rw_mu": f(inputs["rw_mu"]).reshape(1, 4224),
        "rw_w0": f(inputs["rw_w0"]).reshape(1, D),
        "rw_decay_up": f(inputs["rw_decay_up"]).reshape(64, D),
        "rw_a0": f(inputs["rw_a0"]).reshape(1, D),
        "rw_iclr_up": f(inputs["rw_iclr_up"]).reshape(64, D),
        "rw_k_k": f(inputs["rw_k_k"]).reshape(1, D),
        "rw_k_a": f(inputs["rw_k_a"]).reshape(1, D),
        "rw_r_k": f(inputs["rw_r_k"]).reshape(1, D),
        "rw_gn_g": f(inputs["rw_gn_g"]).reshape(1, D),
        "rw_gn_b": f(inputs["rw_gn_b"]).reshape(1, D),
        "w_branch_a": f(inputs["w_branch_a"]).reshape(D, D),
        "w_branch_b": f(inputs["w_branch_b"]).reshape(D, D),
        "w_out": f(inputs["w_out"]).reshape(D, D),
        "norm_post_g": f(inputs["norm_post_g"]).reshape(1, D),
    }
    in_maps = []
    for b in range(8):
        m = dict(common)
        m["x"] = np.ascontiguousarray(x[b])
        in_maps.append(m)
    res = run_bass_kernel_spmd(nc, in_maps, core_ids=list(range(8)))
    out = np.stack([np.asarray(r["y"], dtype=np.float32) for r in res.results], axis=0)
    return out
```

```python
import math
import os
from contextlib import ExitStack
import numpy as np
import concourse.bass as bass
import concourse.mybir as mybir
from concourse.bass_utils import run_bass_kernel_spmd

F32 = mybir.dt.float32
BF16 = mybir.dt.bfloat16
AF = mybir.ActivationFunctionType
ALU = mybir.AluOpType

T = 2048
D = 1024
NCOL = 9344
DSCALE = math.exp(-0.5)
PB0 = 3072
PG0 = 7296


class Region:
    __slots__ = ("name", "w", "r", "excl")

    def __init__(self, name, excl=False):
        self.name = name
        self.w = None
        self.r = []
        self.excl = excl


class Sched:
    ROLL = 30000

    def __init__(self, nc):
        self.nc = nc
        self.eng = {"pe": nc.tensor, "act": nc.scalar, "dve": nc.vector,
                    "pool": nc.gpsimd, "sp": nc.sync}
        self.sem = {}
        self.cnt = {}
        self.nsem = 0
        for e in self.eng:
            self._newsem(e)
        self.waited = {e: {} for e in self.eng}
        self.dslots = {}
        for q in ("sp", "pool"):
            self.dslots[q] = [[self._alloc(f"d_{q}_{i}"), 0] for i in range(8)]
        self.dnext = {"sp": 0, "pool": 0}
        self.ninst = {e: 0 for e in self.eng}

    def _alloc(self, name):
        self.nsem += 1
        return self.nc.alloc_semaphore(name)

    def _newsem(self, e):
        self.sem[e] = self._alloc(f"s_{e}_{self.nsem}")
        self.cnt[e] = 0

    def _need(self, e, tok, is_war=False, for_dma=False):
        if tok is None:
            return
        sem, val, pe, kind = tok
        if kind == "c" and pe == e and not for_dma:
            if e == "pe" or is_war:
                return
        key = sem.num
        if self.waited[e].get(key, 0) >= val:
            return
        self.eng[e].wait_ge(sem, val)
        self.waited[e][key] = val

    def _deps(self, e, reads, writes, for_dma=False):
        for R in reads:
            self._need(e, R.w, for_dma=for_dma)
            if R.excl:
                for t in R.r:
                    if t[2] != e:
                        self._need(e, t)
        for R in writes:
            self._need(e, R.w, for_dma=for_dma)
            for t in R.r:
                self._need(e, t, is_war=True, for_dma=for_dma)

    def _commit(self, tok, reads, writes):
        for R in reads:
            if tok[3] == "c":
                R.r = [t for t in R.r if not (t[3] == "c" and t[2] == tok[2])]
            R.r.append(tok)
        for R in writes:
            R.w = tok
            R.r = []

    def op(self, e, fn, reads=(), writes=(), sig=True):
        self._deps(e, reads, writes)
        inst = fn()
        self.ninst[e] += 1
        if not sig:
            tok = (self.sem[e], self.cnt[e] + 1, e, "c")
            self._commit(tok, reads, writes)
            return tok
        self.cnt[e] += 1
        inst.then_inc(self.sem[e], 1)
        tok = (self.sem[e], self.cnt[e], e, "c")
        self._commit(tok, reads, writes)
        if self.cnt[e] >= self.ROLL:
            self._newsem(e)
        return tok

    def dma(self, q, out, in_, reads=(), writes=()):
        self._deps(q, reads, writes, for_dma=True)
        i = self.dnext[q]
        self.dnext[q] = (i + 1) % len(self.dslots[q])
        slot = self.dslots[q][i]
        if slot[1] > 0:
            self._need(q, (slot[0], slot[1], q, "d"))
        slot[1] += 16
        self.eng[q].dma_start(out=out, in_=in_).then_inc(slot[0], 16)
        tok = (slot[0], slot[1], q, "d")
        self._commit(tok, reads, writes)
        return tok

    def wait_tok(self, e, tok):
        self._need(e, tok)

    def barrier(self):
        engs = list(self.eng)
        snap = {e: (self.sem[e], self.cnt[e]) for e in engs}
        dsn = [(sl[0], sl[1], q) for q in self.dslots for sl in self.dslots[q] if sl[1] > 0]
        for e in engs:
            for e2 in engs:
                if snap[e2][1] > 0:
                    self._need(e, (snap[e2][0], snap[e2][1], e2, "c"), for_dma=True)
            for (sem, val, q) in dsn:
                self._need(e, (sem, val, q, "d"))


class _Stop(Exception):
    pass


def build_program(stop=None):
    nc = bass.Bass("TRN2", target_bir_lowering=False)
    S = Sched(nc)
    try:
        _build(nc, S, stop)
    except _Stop:
        S.barrier()
    return nc, S


def _build(nc, S, stop):
    _dumps = os.environ.get("DUMP", "").split(",")

    def dump(name, tile, regions):
        if name not in _dumps:
            return
        shp = list(tile.shape)
        dt_ = tile.dtype
        d = nc.dram_tensor("dbg_" + name, shp, dt_, kind="ExternalOutput").ap()
        S.dma("sp", d, tile[:] if not isinstance(tile, bass.AP) else tile, list(regions), [])

    def chk(name):
        if stop == name:
            raise _Stop()

    def din(name, shape):
        return nc.dram_tensor(name, shape, F32, kind="ExternalInput").ap()

    x = din("x", [T, D])
    g_pre = din("norm_pre_g", [1, D])
    w_in = din("w_in", [D, NCOL])
    ln_g = din("gm_ln_g", [1, D])
    ln_b = din("gm_ln_b", [1, D])
    w_s = din("gm_w_s", [8, 128, 128])
    b_s = din("gm_b_s", [1, 1024])
    mu = din("rw_mu", [1, 4224])
    w0 = din("rw_w0", [1, D])
    dup = din("rw_decay_up", [64, D])
    a0 = din("rw_a0", [1, D])
    iup = din("rw_iclr_up", [64, D])
    k_k = din("rw_k_k", [1, D])
    k_a = din("rw_k_a", [1, D])
    r_k = din("rw_r_k", [1, D])
    gn_g = din("rw_gn_g", [1, D])
    gn_b = din("rw_gn_b", [1, D])
    w_a = din("w_branch_a", [D, D])
    w_b = din("w_branch_b", [D, D])
    w_o = din("w_out", [D, D])
    g_post = din("norm_post_g", [1, D])
    y = nc.dram_tensor("y", [T, D], F32, kind="ExternalOutput").ap()

    _n = [0]
    stacks = [ExitStack()]

    _bytes = [[0]]
    _peak = [0]

    def sb(shape, dt, name=None):
        _n[0] += 1
        n = 1
        for d_ in shape[1:]:
            n *= d_
        n *= 2 if dt == BF16 else 4
        _bytes[-1][0] += (n + 31) // 32 * 32
        tot = sum(b_[0] for b_ in _bytes)
        _peak[0] = max(_peak[0], tot)
        if os.environ.get("SBSTAT"):
            print(f"  sb {name} {shape} -> total {tot}")
        return stacks[-1].enter_context(nc.sbuf_tensor(f"{name or 'sb'}_{_n[0]}", shape, dt))

    def push():
        stacks.append(ExitStack())
        _bytes.append([0])

    def pop():
        S.barrier()
        stacks.pop().close()
        _bytes.pop()

    def R(name):
        return Region(name)

    PS = nc.alloc_psum_tensor("ps_all", [128, 4096], F32)
    PB = [Region(f"bank{b}", excl=True) for b in range(8)]

    def bank(b, lo=0, hi=512):
        return PS[:, b * 512 + lo: b * 512 + hi]

    brot = [0]
    bankset = [list(range(8))]

    def nextbank(k=1):
        if k == 1 and len(bankset[0]) < 8:
            brot[0] = (brot[0] + 1) % len(bankset[0])
            return bankset[0][brot[0]]
        b = brot[0]
        if b % k:
            b += k - (b % k)
        b %= 8
        brot[0] = (b + k) % 8
        return b

    def mm(out, lhsT, rhs, start=True, stop=True, reads=(), writes=(), sig=True, skip=False):
        return S.op("pe", lambda: nc.tensor.matmul(out, lhsT, rhs, start=start, stop=stop, skip_group_check=skip),
                    reads, writes, sig=sig)

    def act(out, in_, func, reads=(), writes=(), bias=None, scale=None):
        kw = {}
        if bias is not None:
            kw["bias"] = bias
        if scale is not None:
            kw["scale"] = scale
        return S.op("act", lambda: nc.scalar.activation(out=out, in_=in_, func=func, **kw), reads, writes)

    def tt(e, out, in0, in1, op, reads=(), writes=()):
        eng = nc.vector if e == "dve" else nc.gpsimd
        return S.op(e, lambda: eng.tensor_tensor(out=out, in0=in0, in1=in1, op=op), reads, writes)

    def ts(e, out, in0, s1, s2, op0, op1=None, reads=(), writes=()):
        eng = nc.vector if e == "dve" else nc.gpsimd
        if op1 is None:
            return S.op(e, lambda: eng.tensor_scalar(out=out, in0=in0, scalar1=s1, scalar2=None, op0=op0), reads, writes)
        return S.op(e, lambda: eng.tensor_scalar(out=out, in0=in0, scalar1=s1, scalar2=s2, op0=op0, op1=op1), reads, writes)

    def stt(out, in0, scalar, in1, op0, op1, reads=(), writes=()):
        return S.op("dve", lambda: nc.vector.scalar_tensor_tensor(out=out, in0=in0, scalar=scalar, in1=in1, op0=op0, op1=op1), reads, writes)

    def memset(e, ap, val, writes=()):
        eng = nc.vector if e == "dve" else nc.gpsimd
        return S.op(e, lambda: eng.memset(ap, val), (), writes)

    ident_f = sb([128, 128], F32, "ident_f"); r_identf = R("identf")
    ident_b = sb([128, 128], BF16, "ident_b"); r_identb = R("identb")
    I2 = sb([128, 64], BF16, "I2"); r_I2 = R("I2")
    blk1 = sb([128, 128], F32, "blk1"); r_blk1 = R("blk1")
    blk64 = sb([128, 128], F32, "blk64"); r_blk64 = R("blk64")
    ones_row = sb([1, 128], F32, "ones_row"); r_ones = R("ones")
    MASK = sb([128, 320], F32, "MASK"); r_mask = R("mask")
    mask2 = sb([128, 128], F32, "mask2"); r_mask2 = R("mask2")
    segm = sb([128, 512], F32, "segm"); r_segm = R("segm")
    eps_rms = sb([128, 1], F32, "eps_rms"); eps_ln = sb([128, 1], F32, "eps_ln"); eps_gn = sb([128, 1], F32, "eps_gn")
    r_eps = R("eps")

    memset("pool", ident_f[:], 0.0, [r_identf])
    S.op("pool", lambda: nc.gpsimd.affine_select(out=ident_f[:], in_=ident_f[:], pattern=[[-1, 128]],
                                                  compare_op=ALU.not_equal, fill=1.0, base=0, channel_multiplier=1),
         [r_identf], [r_identf])
    S.op("dve", lambda: nc.vector.tensor_copy(out=ident_b[:], in_=ident_f[:]), [r_identf], [r_identb])
    tt("dve", I2[:], ident_f[:, 0:64], ident_f[:, 64:128], ALU.add, [r_identf], [r_I2])
    memset("dve", blk1[:], 0.0, [r_blk1])
    memset("dve", blk1[0:64, 0:64], 1.0, [r_blk1])
    memset("dve", blk1[64:128, 64:128], 1.0, [r_blk1])
    memset("dve", blk64[:], 0.0, [r_blk64])
    memset("dve", blk64[0:64, 0:64], 1.0 / 64, [r_blk64])
    memset("dve", blk64[64:128, 64:128], 1.0 / 64, [r_blk64])
    memset("dve", ones_row[:], 1.0, [r_ones])
    memset("dve", mask2[:], 1.0, [r_mask2])
    memset("dve", mask2[64:128, 0:64], 0.0, [r_mask2])
    memset("dve", segm[:], 1.0, [r_segm])
    memset("dve", segm[:].rearrange("p (c t) -> p c t", t=64)[:, :, 0:1], 0.0, [r_segm])
    memset("dve", eps_rms[:], 1e-6, [r_eps])
    memset("dve", eps_ln[:], 1e-5, [r_eps])
    memset("dve", eps_gn[:], 64e-5, [r_eps])
    memset("pool", MASK[:], 1.0, [r_mask])
    for c0, strict in ((0, True), (64, False), (128, True), (192, False)):
        S.op("pool", lambda c0=c0, strict=strict: nc.gpsimd.affine_select(
            out=MASK[0:64, c0:c0 + 64], in_=MASK[0:64, c0:c0 + 64], pattern=[[1, 64]],
            compare_op=ALU.is_ge, fill=0.0, base=(-1 if strict else 0), channel_multiplier=-1),
            [r_mask], [r_mask])
    S.op("pool", lambda: nc.gpsimd.affine_select(
        out=MASK[0:64, 256:320], in_=MASK[0:64, 256:320], pattern=[[-1, 64]],
        compare_op=ALU.is_ge, fill=0.0, base=-1, channel_multiplier=1), [r_mask], [r_mask])
    S.dma("sp", MASK[64:128, :], MASK[0:64, :], [r_mask], [r_mask])

    chk('const')
    PARAM = sb([128, 128], F32, "PARAM"); r_param = R("param")
    PT = sb([128, 128], F32, "PT"); r_pt = R("pt")
    OM = sb([128, 128], F32, "OM")
    memset("dve", PARAM[:], 0.0, [r_param])
    rows = [(mu, 0, 33), (w0, 33, 8), (a0, 41, 8), (k_k, 49, 8), (k_a, 57, 8), (r_k, 65, 8),
            (gn_g, 73, 8), (gn_b, 81, 8), (ln_g, 89, 8)]
    for ap_, r0, n in rows:
        S.dma("sp", PARAM[r0:r0 + n, :], ap_.rearrange("o (r c) -> (o r) c", c=128), [], [r_param])
    b0 = nextbank()
    mm(bank(b0, 0, 128), PARAM[:], ident_f[:], reads=[r_param, r_identf], writes=[PB[b0]])
    S.op("dve", lambda: nc.vector.tensor_copy(out=PT[:], in_=bank(b0, 0, 128)), [PB[b0]], [r_pt])
    ts("dve", OM[:], PT[:], -1.0, 1.0, ALU.mult, ALU.add, [r_pt], [r_pt])
    C_MU, C_W0, C_A0, C_KK, C_KA, C_RK, C_GG, C_GB, C_LG = 0, 33, 41, 49, 57, 65, 73, 81, 89

    def pcol(c):
        return PT[:, c:c + 1]

    def ocol(c):
        return OM[:, c:c + 1]

    r_gbc = R("gbc")
    r_row = R("row")

    def bcast_row(src_ap, dst, r_dst):
        push()
        ROW = sb([1, 1024], F32, "ROW")
        S.dma("sp", ROW[:], src_ap, [], [r_row])
        for h in range(2):
            b = nextbank()
            mm(bank(b), ones_row[0:1, 0:128], ROW[0:1, h * 512:(h + 1) * 512], reads=[r_ones, r_row], writes=[PB[b]])
            S.op("dve", lambda b=b, h=h: nc.vector.tensor_copy(out=dst[:, h * 512:(h + 1) * 512], in_=bank(b)), [PB[b]], [r_dst])
        pop()

    chk('param')
    hT = sb([128, 8, T], BF16, "hT")
    r_hT = [R(f"hT{i}") for i in range(16)]
    STAT = [sb([128, 16], F32, f"STAT{i}") for i in range(4)]; r_STAT = [R(f"stat{i}") for i in range(4)]
    NWB_P = 2
    WBUF = [sb([128, 8, 128], BF16, f"WB{i}") for i in range(NWB_P)]
    r_WB = [R(f"wb{i}") for i in range(NWB_P)]

    def more_wbufs(n):
        for i in range(n):
            WBUF.append(sb([128, 8, 128], BF16, f"WBx{len(WBUF)}"))
            r_WB.append(R(f"wbx{len(r_WB)}"))

    def drop_wbufs():
        del WBUF[NWB_P:]
        del r_WB[NWB_P:]
        wrot[0] = 0
    VN = sb([128, 16, 1024], BF16, "VN")
    r_VN = [R(f"vn{i}") for i in range(16)]
    YA = sb([128, 8, T], BF16, "YA")
    r_YA = [[R(f"ya{j}_{q}") for q in range(4)] for j in range(8)]

    push()
    WMT = sb([128, 8, 128], BF16, "WMT")
    BIAS = sb([128, 8, 128], F32, "BIAS")
    r_wmt = [R(f"wmt{g}") for g in range(8)]
    WV = sb([128, 8, 1024], BF16, "WV"); r_wv = R("wv")
    for kc in range(8):
        S.dma("pool", WV[:, kc, :], w_in[kc * 128:(kc + 1) * 128, 1024:2048], [], [r_wv])
    push()
    GBC = sb([128, 1024], F32, "GBC")
    bcast_row(g_pre, GBC, r_gbc)
    LNB = sb([128, 1024], F32, "LNB"); r_lnb = R("lnb")
    BSR = sb([1, 1024], F32, "BSR"); r_bsr = R("bsr")
    WS32 = sb([128, 128], F32, "WS32"); r_ws32 = R("ws32")
    WMT32 = sb([128, 8, 128], F32, "WMT32")
    bcast_row(ln_b, LNB, r_lnb)
    S.dma("sp", BSR[:], b_s, [], [r_bsr])
    for g in range(8):
        S.dma("sp", WS32[:], w_s[g], [], [r_ws32])
        b = nextbank()
        mm(bank(b, 0, 128), WS32[:], ident_f[:], reads=[r_ws32, r_identf], writes=[PB[b]])
        tt("dve", WMT32[:, g, :], bank(b, 0, 128), mask2[:], ALU.mult, [PB[b], r_mask2], [r_wmt[g]])
        act(WMT[:, g, :], WMT32[:, g, :], AF.Copy, [r_wmt[g]], [r_wmt[g]])
        b2 = nextbank()
        mm(bank(b2, 0, 128), LNB[:, g * 128:(g + 1) * 128], WMT32[:, g, :], start=True, stop=False,
           reads=[r_lnb, r_wmt[g]], writes=[PB[b2]])
        mm(bank(b2, 0, 128), ones_row[0:1, 0:128], BSR[0:1, g * 128:(g + 1) * 128], start=False, stop=True,
           reads=[r_ones, r_bsr], writes=[PB[b2]])
        S.op("dve", lambda g=g, b2=b2: nc.vector.tensor_copy(out=BIAS[:, g, :], in_=bank(b2, 0, 128)), [PB[b2]], [r_wmt[g]])

    XT = [sb([128, D], F32, f"XT{i}") for i in range(4)]; r_XT = [R(f"xt{i}") for i in range(4)]
    XS = [sb([128, D], BF16, f"XS{i}") for i in range(2)]; r_XS = [R("xs0"), R("xs1")]

    def rstd_of(var_ap, eps_t, out_ap, tmp_ap, reads, rgn):
        act(tmp_ap, var_ap, AF.Sqrt, reads + [r_eps], [rgn], bias=eps_t[:], scale=1.0)
        S.op("dve", lambda: nc.vector.reciprocal(out=out_ap, in_=tmp_ap), [rgn], [rgn])

    A_bank = {}

    def A_s0(i):
        S.dma("sp", XT[i % 4][:], x[i * 128:(i + 1) * 128, :], [], [r_XT[i % 4]])

    def A_s1(i):
        k, k4 = i % 4, i % 4
        st = STAT[k4]
        S.op("dve", lambda: nc.vector.bn_stats(out=st[:, 0:6], in_=XT[k][:, 0:512]), [r_XT[k]], [r_STAT[k4]])
        S.op("dve", lambda: nc.vector.bn_stats(out=st[:, 6:12], in_=XT[k][:, 512:1024]), [r_XT[k]], [r_STAT[k4]])
        S.op("dve", lambda: nc.vector.bn_aggr(out=st[:, 12:14], in_=st[:, 0:12]), [r_STAT[k4]], [r_STAT[k4]])
        stt(st[:, 14:15], st[:, 12:13], st[:, 12:13], st[:, 13:14], ALU.mult, ALU.add, [r_STAT[k4]], [r_STAT[k4]])
        act(st[:, 14:15], st[:, 14:15], AF.Sqrt, [r_STAT[k4], r_eps], [r_STAT[k4]], bias=eps_rms[:], scale=1.0)

    def A_s2(i):
        k, k4, k2 = i % 4, i % 4, i % 2
        st = STAT[k4]
        S.op("dve", lambda: nc.vector.reciprocal(out=st[:, 15:16], in_=st[:, 14:15]), [r_STAT[k4]], [r_STAT[k4]])
        stt(XS[k2][:], XT[k][:], st[:, 15:16], GBC[:], ALU.mult, ALU.mult, [r_XT[k], r_STAT[k4], r_gbc], [r_XS[k2]])
        b = nextbank(2)
        A_bank[i] = b
        for j in range(8):
            bb = b + (j // 4)
            mm(bank(bb, (j % 4) * 128, (j % 4) * 128 + 128), XS[k2][:, j * 128:(j + 1) * 128], ident_b[:],
               reads=[r_XS[k2], r_identb], writes=[PB[bb]], sig=(j % 4 == 3))

    def A_s3(i):
        b = A_bank[i]
        for h in range(2):
            act(hT[:, 4 * h:4 * h + 4, i * 128:(i + 1) * 128],
                bank(b + h).rearrange("p (j t) -> p j t", t=128), AF.Copy, [PB[b + h]], [r_hT[i]])

    for step in range(16 + 3):
        if step < 16:
            A_s0(step)
        if 0 <= step - 1 < 16:
            A_s1(step - 1)
        if 0 <= step - 2 < 16:
            A_s2(step - 2)
        if 0 <= step - 3 < 16:
            A_s3(step - 3)

    chk('A')
    pop()

    def hT_regions(q):
        return r_hT[4 * q:4 * q + 4]

    wrot = [0]

    def load_w(src, c0):
        i = wrot[0]
        wrot[0] = (i + 1) % len(WBUF)
        S.dma("pool", WBUF[i][:], src[:, c0:c0 + 128].rearrange("(kc p) c -> p kc c", p=128), [], [r_WB[i]])
        return i

    def proj_quad(wi, q, b, rhs_src=None, rhs_regions=None):
        for kc in range(8):
            if rhs_src is None:
                rhs = hT[:, kc, q * 512:(q + 1) * 512]
                rr = hT_regions(q)
            else:
                rhs = rhs_src[:, kc, q * 512:(q + 1) * 512]
                rr = rhs_regions(q)
            mm(bank(b), WBUF[wi][:, kc, :], rhs, start=(kc == 0), stop=(kc == 7),
               reads=[r_WB[wi]] + rr, writes=[PB[b]], sig=(kc == 7))

    push()
    more_wbufs(4)
    VG = [sb([128, 1024], F32, f"VG{i}") for i in range(3)]; r_VG = [R(f"vg{i}") for i in range(3)]
    UG = [sb([128, T], F32, f"UG{i}") for i in range(2)]; r_UG = [R("ug0"), R("ug1")]
    ZS = [sb([128, T], F32, f"ZSa{i}") for i in range(2)]; r_ZS = [R("zsa0"), R("zsa1")]
    TMPS = [sb([128, 512], F32, f"TMPS{i}") for i in range(2)]; r_TMPS = [R("tmps0"), R("tmps1")]
    def B_s1(i):
        k = i % 3
        b = nextbank(2)
        for h in range(2):
            for kc in range(8):
                mm(bank(b + h), hT[:, kc, i * 128:(i + 1) * 128], WV[:, kc, h * 512:(h + 1) * 512],
                   start=(kc == 0), stop=(kc == 7), reads=[r_hT[i], r_wv], writes=[PB[b + h]], sig=(kc == 7))
            act(VG[k][:, h * 512:(h + 1) * 512], bank(b + h), AF.Gelu_apprx_tanh, [PB[b + h]], [r_VG[k]])

    def B_s2(i):
        k, k4 = i % 3, i % 4
        st = STAT[k4]
        S.op("dve", lambda: nc.vector.bn_stats(out=st[:, 0:6], in_=VG[k][:, 0:512]), [r_VG[k]], [r_STAT[k4]])
        S.op("dve", lambda: nc.vector.bn_stats(out=st[:, 6:12], in_=VG[k][:, 512:1024]), [r_VG[k]], [r_STAT[k4]])
        S.op("dve", lambda: nc.vector.bn_aggr(out=st[:, 12:14], in_=st[:, 0:12]), [r_STAT[k4]], [r_STAT[k4]])
        act(st[:, 14:15], st[:, 13:14], AF.Sqrt, [r_STAT[k4], r_eps], [r_STAT[k4]], bias=eps_ln[:], scale=1.0)

    def B_s3(i):
        k, k4 = i % 3, i % 4
        st = STAT[k4]
        S.op("dve", lambda: nc.vector.reciprocal(out=st[:, 15:16], in_=st[:, 14:15]), [r_STAT[k4]], [r_STAT[k4]])
        stt(st[:, 14:15], st[:, 12:13], -1.0, st[:, 15:16], ALU.mult, ALU.mult, [r_STAT[k4]], [r_STAT[k4]])
        ts("dve", VN[:, i, :], VG[k][:], st[:, 15:16], st[:, 14:15], ALU.mult, ALU.add, [r_VG[k], r_STAT[k4]], [r_VN[i]])

    for step in range(16 + 2):
        if step < 16:
            B_s1(step)
        if 0 <= step - 1 < 16:
            B_s2(step - 1)
        if 0 <= step - 2 < 16:
            B_s3(step - 2)

    chk('B1')
    wnext = (load_w(w_in, 0), load_w(w_in, 2048))
    for j in range(8):
        k = j % 2
        wu, wz = wnext
        if j < 7:
            wnext = (load_w(w_in, (j + 1) * 128), load_w(w_in, 2048 + (j + 1) * 128))
        for q in range(4):
            b = nextbank()
            proj_quad(wu, q, b)
            act(UG[k][:, q * 512:(q + 1) * 512], bank(b), AF.Gelu_apprx_tanh, [PB[b]], [r_UG[k]])
        for q in range(4):
            b = nextbank()
            proj_quad(wz, q, b)
            act(ZS[k][:, q * 512:(q + 1) * 512], bank(b), AF.Silu, [PB[b]], [r_ZS[k]])
        tt("pool", UG[k][:], UG[k][:], ZS[k][:], ALU.mult, [r_UG[k], r_ZS[k]], [r_UG[k]])
        for q in range(4):
            b = nextbank()
            for ii in range(4):
                i = 4 * q + ii
                mm(bank(b, ii * 128, ii * 128 + 128), VN[:, i, j * 128:(j + 1) * 128], WMT[:, j, :],
                   reads=[r_VN[i], r_wmt[j]], writes=[PB[b]], sig=(ii == 3))
            kk_ = q % 2
            stt(TMPS[kk_][:].rearrange("p (a t) -> p a t", t=128), bank(b).rearrange("p (a t) -> p a t", t=128),
                pcol(C_LG + j), BIAS[:, j:j + 1, :].to_broadcast([128, 4, 128]), ALU.mult, ALU.add,
                [PB[b], r_pt, r_wmt[j]], [r_TMPS[kk_]])
            tt("dve", YA[:, j, q * 512:(q + 1) * 512], TMPS[kk_][:], UG[k][:, q * 512:(q + 1) * 512], ALU.mult,
               [r_TMPS[kk_], r_UG[k]], [r_YA[j][q]])

    chk('B2')
    drop_wbufs()
    pop()
    pop()
    push()
    YB = VN
    YBv = VN[:].rearrange("p a b -> p (a b)").rearrange("p (j t) -> p j t", t=T)
    r_YB = [[R(f"yb{j}_{q}") for q in range(4)] for j in range(8)]
    r_VN_all = r_VN

    TA = sb([128, T], BF16, "TA"); r_TA = R("ta")
    DUP = sb([128, D], BF16, "DUP"); r_dup = R("dup")
    S.dma("pool", DUP[0:64, :], dup, [], [r_dup])
    S.dma("pool", DUP[64:128, :], iup, [], [r_dup])
    push()
    SHB = sb([128, 513], F32, "SHB"); r_SHB = R("shb")
    CAR = sb([128, 8], F32, "CAR")
    XL = sb([128, 512], F32, "XL"); r_XL = R("xl")

    def shift_evac(b, n, kd, OUT, r_OUT, first):
        if first:
            memset("pool", SHB[:, 0:1], 0.0, [r_SHB])
        else:
            S.op("pool", lambda: nc.gpsimd.tensor_copy(out=SHB[:, 0:1], in_=CAR[:, kd:kd + 1]), [r_SHB], [r_SHB])
        act(SHB[:, 1:513], bank(b), AF.Identity, [PB[b], r_pt, r_SHB], [r_SHB], scale=pcol(C_MU + n))
        stt(OUT, bank(b), ocol(C_MU + n), SHB[:, 0:512], ALU.mult, ALU.add, [PB[b], r_pt, r_SHB], [r_OUT])
        S.op("pool", lambda: nc.gpsimd.tensor_copy(out=CAR[:, kd:kd + 1], in_=SHB[:, 512:513]), [r_SHB], [r_SHB])

    wl = load_w(w_in, PB0 + 4096)
    for q in range(4):
        b = nextbank()
        proj_quad(wl, q, b)
        shift_evac(b, 32, 4, XL[:], r_XL, q == 0)
        act(TA[0:64, q * 512:(q + 1) * 512], XL[0:64, :], AF.Tanh, [r_XL], [r_TA])
        act(TA[64:128, q * 512:(q + 1) * 512], XL[64:128, :], AF.Copy, [r_XL], [r_TA])

    chk('C0')
    pop()
    W4 = [sb([128, 8, 128], BF16, f"W4_{kd}") for kd in range(4)]
    r_W4 = [R(f"w4_{kd}") for kd in range(4)]

    def scr(name, dt=F32, w=512, n=1):
        return [sb([128, w], dt, f"{name}{s}") for s in range(n)], [R(f"{name}{s}") for s in range(n)]

    Xr, r_Xr = scr("Xr"); Xk, r_Xk = scr("Xk"); Xv, r_Xv = scr("Xv")
    SG, r_SG = scr("SG"); AAt, r_AA = scr("AA"); CC, r_CC = scr("CC")
    Wi, r_Wi = scr("Wi", BF16); Wx, r_Wx = scr("Wx", BF16)
    KQ, r_KQ = scr("KQ")
    CX, r_CX = SG, r_SG
    BH, r_BH = SG, r_SG
    KKN, r_KKN = CC, r_CC
    RS, r_RS = KQ, r_KQ
    DD, r_DD = scr("DD"); DSQ, r_DSQ = scr("DSQ")
    TA_, r_TA_ = DD, r_DD
    KM, r_KM = DSQ, r_DSQ
    RK, r_RK = DD, r_DD
    SH = [sb([128, 513], F32, f"SH{kd}") for kd in range(4)]; r_SH = [R(f"sh{kd}") for kd in range(4)]
    Xz, r_Xz = scr("Xz", n=2); Wt, r_Wt = scr("Wt", n=2); BON, r_BON = scr("BON", n=2)
    AR, r_AR = scr("AR", BF16, 1024, n=2)
    BBt, r_BB = scr("BBt", BF16, n=2); KKt, r_KK = scr("KKt", BF16, n=2)
    BBAR, r_BBAR = scr("BBAR", BF16, n=2); KBAR, r_KBAR = scr("KBAR", BF16, n=2); VB, r_VB = scr("VB", BF16, n=2)
    OF, r_OF = scr("OF", n=2)
    NG = 4
    CH = [sb([128, NG, 320], BF16, f"CH{g}") for g in range(2)]; r_CH = [R(f"ch{g}") for g in range(2)]
    GM = [sb([128, NG, 320], BF16, f"GM{g}") for g in range(2)]; r_GM = [R(f"gm{g}") for g in range(2)]
    RB = [[sb([128, NG, 192], BF16, f"RB{g}_{s}") for s in range(2)] for g in range(2)]
    r_RB = [[R(f"rb{g}_{s}") for s in range(2)] for g in range(2)]
    TTF = [sb([128, NG, 64], BF16, f"TTF{g}") for g in range(2)]; r_TTF = [R(f"ttf{g}") for g in range(2)]
    US = [sb([128, NG, 128], BF16, f"US{g}") for g in range(2)]; r_US = [R(f"us{g}") for g in range(2)]
    PO = [sb([128, NG, 128], BF16, f"PO{g}") for g in range(2)]; r_PO = [R(f"po{g}") for g in range(2)]
    QQ = [sb([128, NG, 64], F32, f"QQ{g}") for g in range(2)]; r_QQ = [R(f"qq{g}") for g in range(2)]
    SWQ = [sb([128, 64], F32, f"SWQ{i}") for i in range(2)]; r_SWQ = [R(f"swq{i}") for i in range(2)]
    SQ = sb([128, 9, 64], BF16, "SQ"); r_SQ = [R(f"sq{i}") for i in range(9)]

    HALF = ((0, 64), (64, 128))
    NCH = [Region(f"nch{g}", excl=True) for g in range(2)]
    NOO = [Region(f"noo{g}", excl=True) for g in range(2)]
    bankset[0] = [6, 7]

    def load_pair_weights(p):
        for kd in range(4):
            S.dma("pool", W4[kd][:], w_in[:, PB0 + kd * 1024 + p * 128: PB0 + kd * 1024 + (p + 1) * 128]
                  .rearrange("(kc p) c -> p kc c", p=128), [], [r_W4[kd]])

    def v3(ap):
        return ap.rearrange("p (c t) -> p c t", t=64)

    iters = [(p, q) for p in range(8) for q in range(4)]

    def C1_gen(n):
        p, q = iters[n]
        s = n % 2
        B6, B7 = 6, 7
        if q == 0:
            load_pair_weights(p)

        def proj(kd, b):
            for kc in range(8):
                mm(bank(b), W4[kd][:, kc, :], hT[:, kc, q * 512:(q + 1) * 512], start=(kc == 0), stop=(kc == 7),
                   reads=[r_W4[kd]] + hT_regions(q), writes=[PB[b]], sig=(kc == 7))

        def sh_act(kd, b):
            act(SH[kd][:, 1:513], bank(b), AF.Identity, [PB[b], r_pt, r_SH[kd]], [r_SH[kd]], scale=pcol(C_MU + kd * 8 + p))

        def sh_dve(kd, b, OUT, r_OUT):
            stt(OUT, bank(b), ocol(C_MU + kd * 8 + p), SH[kd][:, 0:512], ALU.mult, ALU.add, [PB[b], r_pt, r_SH[kd]], [r_OUT])

        def sh_carry(kd):
            S.op("pool", lambda: nc.gpsimd.tensor_copy(out=SH[kd][:, 0:1], in_=SH[kd][:, 512:513]), [r_SH[kd]], [r_SH[kd]])

        ARv = AR[s][:].rearrange("p (c two t) -> p c two t", two=2, t=64)
        wcb = v3(Wt[s][:])[:, :, 63:64].to_broadcast([128, 8, 64])
        if q == 0:
            for kd in range(4):
                memset("pool", SH[kd][:, 0:1], 0.0, [r_SH[kd]])
        proj(0, B6); proj(1, B7)
        yield
        sh_act(0, B6); sh_act(1, B7)
        yield
        sh_dve(0, B6, Xr[0][:], r_Xr[0]); sh_dve(1, B7, Xk[0][:], r_Xk[0])
        yield "cut"
        proj(2, B6); proj(3, B7); sh_carry(0); sh_carry(1)
        act(KQ[0][:], Xk[0][:], AF.Square, [r_Xk[0], r_pt], [r_KQ[0]], scale=pcol(C_KK + p))
        yield
        sh_act(2, B6); sh_act(3, B7)
        yield
        sh_dve(2, B6, Xv[0][:], r_Xv[0]); sh_dve(3, B7, Xz[s][:], r_Xz[s])
        yield "cut"
        mm(bank(B6), DUP[0:64, p * 128:(p + 1) * 128], TA[0:64, q * 512:(q + 1) * 512], reads=[r_dup, r_TA], writes=[PB[B6]])
        mm(bank(B7), DUP[64:128, p * 128:(p + 1) * 128], TA[64:128, q * 512:(q + 1) * 512], reads=[r_dup, r_TA], writes=[PB[B7]])
        sh_carry(2); sh_carry(3)
        yield
        act(SG[0][:], bank(B6), AF.Sigmoid, [PB[B6], r_pt], [r_SG[0]], bias=pcol(C_W0 + p), scale=1.0)
        act(AAt[0][:], bank(B7), AF.Sigmoid, [PB[B7], r_pt], [r_AA[0]], bias=pcol(C_A0 + p), scale=1.0)
        S.op("pool", lambda: nc.gpsimd.tensor_copy(out=VB[s][:], in_=Xv[0][:]), [r_Xv[0]], [r_VB[s]])
        yield "cut"
        mm(bank(B6), blk1[:], KQ[0][:], reads=[r_blk1, r_KQ[0]], writes=[PB[B6]])
        S.op("dve", lambda: nc.vector.tensor_tensor_scan(out=CC[0][:], data0=segm[:], data1=SG[0][:], initial=0.0,
                                                         op0=ALU.mult, op1=ALU.add), [r_segm, r_SG[0]], [r_CC[0]])
        act(Xz[s][:], Xz[s][:], AF.Silu, [r_Xz[s]], [r_Xz[s]])
        ts("dve", TA_[0][:], AAt[0][:], pcol(C_KA + p), ocol(C_KA + p), ALU.mult, ALU.add, [r_AA[0], r_pt], [r_TA_[0]])
        yield
        ts("dve", RS[0][:], bank(B6), 1e-12, None, ALU.max, None, [PB[B6]], [r_RS[0]])
        act(Wt[s][:], CC[0][:], AF.Exp, [r_CC[0]], [r_Wt[s]], scale=-DSCALE)
        act(Wi[0][:], CC[0][:], AF.Exp, [r_CC[0]], [r_Wi[0]], scale=DSCALE)
        tt("pool", CX[0][:], CC[0][:], SG[0][:], ALU.subtract, [r_CC[0], r_SG[0]], [r_CX[0]])
        tt("pool", KM[0][:], Xk[0][:], TA_[0][:], ALU.mult, [r_Xk[0], r_TA_[0]], [r_KM[0]])
        yield
        act(Wx[0][:], CX[0][:], AF.Exp, [r_CX[0]], [r_Wx[0]], scale=-DSCALE)
        act(RS[0][:], RS[0][:], AF.Ln, [r_RS[0]], [r_RS[0]])
        tt("pool", ARv[:, :, 1, :], v3(Xr[0][:]), v3(Wt[s][:]), ALU.mult, [r_Xr[0], r_Wt[s]], [r_AR[s]])
        tt("pool", KKt[s][:], KM[0][:], Wi[0][:], ALU.mult, [r_KM[0], r_Wi[0]], [r_KK[s]])
        stt(RK[0][:], Xr[0][:], pcol(C_RK + p), KM[0][:], ALU.mult, ALU.mult, [r_Xr[0], r_pt, r_KM[0]], [r_RK[0]])
        yield "cut"
        mm(bank(B7), blk1[:], RK[0][:], reads=[r_blk1, r_RK[0]], writes=[PB[B7]])
        act(RS[0][:], RS[0][:], AF.Exp, [r_RS[0]], [r_RS[0]], scale=-0.5)
        tt("pool", v3(KBAR[s][:]), v3(KKt[s][:]), wcb, ALU.mult, [r_KK[s], r_Wt[s]], [r_KBAR[s]])
        yield
        tt("dve", BON[s][:], bank(B7), Xv[0][:], ALU.mult, [PB[B7], r_Xv[0]], [r_BON[s]])
        stt(KKN[0][:], Xk[0][:], pcol(C_KK + p), RS[0][:], ALU.mult, ALU.mult, [r_Xk[0], r_pt, r_RS[0]], [r_KKN[0]])
        yield
        stt(ARv[:, :, 0, :], v3(KKN[0][:]), -1.0, v3(Wx[0][:]), ALU.mult, ALU.mult, [r_KKN[0], r_Wx[0]], [r_AR[s]])
        tt("pool", BH[0][:], KKN[0][:], AAt[0][:], ALU.mult, [r_KKN[0], r_AA[0]], [r_BH[0]])
        yield
        tt("pool", BBt[s][:], BH[0][:], Wi[0][:], ALU.mult, [r_BH[0], r_Wi[0]], [r_BB[s]])
        yield
        tt("pool", v3(BBAR[s][:]), v3(BBt[s][:]), wcb, ALU.mult, [r_BB[s], r_Wt[s]], [r_BBAR[s]])
        yield

    def C3_gen(n):
        p, q = iters[n]
        s = n % 2
        b1, b2 = 6, 7
        mm(bank(b1), blk64[:], OF[s][:], reads=[r_blk64, r_OF[s]], writes=[PB[b1]])
        yield
        stt(DD[0][:], bank(b1), -1.0, OF[s][:], ALU.mult, ALU.add, [r_OF[s], PB[b1]], [r_DD[0]])
        yield
        act(DSQ[0][:], DD[0][:], AF.Square, [r_DD[0]], [r_DSQ[0]])
        yield "cut"
        mm(bank(b2), blk64[:], DSQ[0][:], reads=[r_blk64, r_DSQ[0]], writes=[PB[b2]])
        yield
        act(DSQ[0][:], bank(b2), AF.Ln, [PB[b2], r_eps], [r_DSQ[0]], bias=eps_gn[:], scale=1.0)
        yield
        act(DSQ[0][:], DSQ[0][:], AF.Exp, [r_DSQ[0]], [r_DSQ[0]], scale=-0.5)
        yield
        stt(DD[0][:], DD[0][:], pcol(C_GG + p), DSQ[0][:], ALU.mult, ALU.mult, [r_DD[0], r_pt, r_DSQ[0]], [r_DD[0]])
        yield
        stt(DD[0][:], DD[0][:], pcol(C_GB + p), BON[s][:], ALU.add, ALU.add, [r_DD[0], r_pt, r_BON[s]], [r_DD[0]])
        yield
        tt("dve", YBv[:, p, q * 512:(q + 1) * 512], DD[0][:], Xz[s][:], ALU.mult, [r_DD[0], r_Xz[s]],
           [r_YB[p][q]] + r_VN_all)
        yield "cut"

    def run_bg(gen, k):
        for _ in range(k):
            try:
                if next(gen) == "cut":
                    return
            except StopIteration:
                return

    def chain_gens(gens):
        for g_ in gens:
            yield from g_

    def grp(calls):
        los = [c_ for c_ in calls if c_[0].base_partition() == 0]
        his = [c_ for c_ in calls if c_[0].base_partition() != 0]
        if len(los) == len(his) and len(los) > 0:
            calls = [c_ for pair_ in zip(los, his) for c_ in pair_]
        for n_, c_ in enumerate(calls):
            o_, l_, r_, st_, sp_, rd_, wr_ = c_[:7]
            mm(o_, l_, r_, start=st_, stop=sp_, reads=rd_, writes=wr_, sig=(n_ == len(calls) - 1),
               skip=(len(c_) > 7 and c_[7]))

    def C2(n, bg):
        p, q = iters[n]
        s = n % 2
        GRP = (0, 1)
        gb0 = (0, 3)

        def Wc(g, j, c0, c1, lo=0, hi=128):
            base = gb0[g] * 512 + j * 256
            return PS[lo:hi, base + c0: base + c1]

        def Nc(g, j, c0, c1, lo=0, hi=128):
            base = (gb0[g] + 2) * 512 + j * 128
            return PS[lo:hi, base + c0: base + c1]

        def Wall(g, c0, c1):
            return PS[:, gb0[g] * 512: gb0[g] * 512 + 1024].rearrange("p (j w) -> p j w", w=256)[:, :, c0:c1]

        def Nall(g, c0, c1):
            return PS[:, (gb0[g] + 2) * 512: (gb0[g] + 3) * 512].rearrange("p (j w) -> p j w", w=128)[:, :, c0:c1]

        def rW(g):
            return [PB[gb0[g]], PB[gb0[g] + 1]]

        def rN(g):
            return [NCH[g], NOO[g]]

        def wb(g, j):
            return [PB[gb0[g] + j // 2]]

        def ck(g, j):
            return g * NG + j

        def cp(g, out, in_, reads, writes):
            if g == 0:
                act(out, in_, AF.Copy, reads, writes)
            else:
                S.op("dve", lambda: nc.vector.tensor_copy(out=out, in_=in_), reads, writes)

        def TR_calls(nn, g, j):
            sN = nn % 2
            c = ck(g, j)
            calls = []
            srcs = ((BBAR[sN], r_BBAR[sN], None), (KBAR[sN], r_KBAR[sN], None), (VB[sN], r_VB[sN], None), (AR[sN], r_AR[sN], 0))
            for si, (src, rs_, arsel) in enumerate(srcs):
                for (lo, hi) in HALF:
                    l = src[lo:hi, c * 64:(c + 1) * 64] if arsel is None else src[lo:hi, c * 128:c * 128 + 64]
                    calls.append((Wc(g, j, si * 64, si * 64 + 64, lo, hi), l, ident_b[lo:hi, lo:hi], True, True,
                                  [rs_, r_identb], wb(g, j)))
            return calls

        if n == 0:
            for g in GRP:
                for j in range(NG):
                    grp(TR_calls(0, g, j))
            for g in GRP:
                cp(g, CH[g][:, :, 0:256], Wall(g, 0, 256), rW(g), [r_CH[g]])
        run_bg(bg, 5)
        for g in GRP:
            for j in range(NG):
                c = ck(g, j)
                calls = []
                for (lo, hi) in HALF:
                    arg = AR[s][lo:hi, c * 128:(c + 1) * 128]
                    calls.append((Wc(g, j, 0, 128, lo, hi), BBt[s][lo:hi, c * 64:(c + 1) * 64], arg, True, True,
                                  [r_BB[s], r_AR[s]], wb(g, j)))
                    calls.append((Wc(g, j, 128, 256, lo, hi), KKt[s][lo:hi, c * 64:(c + 1) * 64], arg, True, True,
                                  [r_KK[s], r_AR[s]], wb(g, j)))
                    calls.append((Nc(g, j, 0, 64, lo, hi), AR[s][lo:hi, c * 128:c * 128 + 64], BBt[s][lo:hi, c * 64:(c + 1) * 64],
                                  True, True, [r_BB[s], r_AR[s]], rN(g)))
                grp(calls)
        for g in GRP:
            mk = lambda c0, c1: MASK[:, c0:c1].unsqueeze(1).to_broadcast([128, NG, c1 - c0])
            tt("dve", GM[g][:, :, 0:256], Wall(g, 0, 256), mk(0, 256), ALU.mult, rW(g) + [r_mask], [r_GM[g]])
            tt("dve", RB[g][0][:, :, 128:192], Nall(g, 0, 64), mk(256, 320), ALU.mult, rN(g) + [r_mask], [r_RB[g][0]])
        run_bg(bg, 5)
        for g in GRP:
            for j in range(NG):
                calls = []
                for (lo, hi) in HALF:
                    calls.append((Nc(g, j, 64, 128, lo, hi), GM[g][lo:hi, j, 128:192], CH[g][lo:hi, j, 128:192], True, True,
                                  [r_GM[g], r_CH[g]], rN(g)))
                grp(calls)
        for g in GRP:
            cp(g, CH[g][:, :, 256:320], Nall(g, 64, 128), rN(g), [r_CH[g]])
        run_bg(bg, 5)
        for r in range(1, 6):
            cur, nxt = (r - 1) % 2, r % 2
            for g in GRP:
                for j in range(NG):
                    calls = []
                    for (lo, hi) in HALF:
                        Ncur = RB[g][cur][lo:hi, j, 128:192]
                        Zcur = RB[g][cur][lo:hi, j, 64:128] if r > 1 else GM[g][lo:hi, j, 0:64]
                        rz = [r_RB[g][cur]] if r > 1 else [r_GM[g], r_RB[g][cur]]
                        if r == 1:
                            calls.append((Wc(g, j, 0, 64, lo, hi), ident_b[lo:hi, lo:hi], I2[lo:hi, :], True, False,
                                          [r_identb, r_I2], wb(g, j)))
                            calls.append((Wc(g, j, 0, 64, lo, hi), ident_b[lo:hi, lo:hi], Zcur, False, True,
                                          [r_identb] + rz, wb(g, j)))
                            calls.append((Wc(g, j, 64, 128, lo, hi), Ncur, Zcur, True, True, rz, wb(g, j)))
                        else:
                            calls.append((Wc(g, j, 0, 64, lo, hi), ident_b[lo:hi, lo:hi], RB[g][cur][lo:hi, j, 0:64], True, False,
                                          [r_identb, r_RB[g][cur]], wb(g, j), True))
                            calls.append((Wc(g, j, 0, 128, lo, hi), Ncur, RB[g][cur][lo:hi, j, 0:128], False, True, [r_RB[g][cur]], wb(g, j), True))
                        calls.append((Wc(g, j, 128, 192, lo, hi), Zcur, Ncur, True, True, rz, wb(g, j)))
                    grp(calls)
            for g in GRP:
                cp(g, RB[g][nxt][:, :, 0:192], Wall(g, 0, 192), rW(g), [r_RB[g][nxt]])
            run_bg(bg, 5)
        fin = 5 % 2
        for g in GRP:
            for j in range(NG):
                calls = []
                for (lo, hi) in HALF:
                    calls.append((Wc(g, j, 0, 64, lo, hi), ident_b[lo:hi, lo:hi], RB[g][fin][lo:hi, j, 0:64], True, False,
                                  [r_identb, r_RB[g][fin]], wb(g, j)))
                    calls.append((Wc(g, j, 0, 64, lo, hi), RB[g][fin][lo:hi, j, 128:192], RB[g][fin][lo:hi, j, 0:64], False, True,
                                  [r_RB[g][fin]], wb(g, j)))
                grp(calls)
        for g in GRP:
            cp(g, TTF[g][:], Wall(g, 0, 64), rW(g), [r_TTF[g]])
        run_bg(bg, 5)
        for g in GRP:
            for j in range(NG):
                calls = []
                for (lo, hi) in HALF:
                    calls.append((Wc(g, j, 0, 128, lo, hi), TTF[g][lo:hi, j, :], CH[g][lo:hi, j, 192:320], True, True,
                                  [r_TTF[g], r_CH[g]], wb(g, j)))
                grp(calls)
        for g in GRP:
            cp(g, US[g][:], Wall(g, 0, 128), rW(g), [r_US[g]])
        run_bg(bg, 5)
        for g in GRP:
            for j in range(NG):
                calls = []
                for (lo, hi) in HALF:
                    calls.append((Wc(g, j, 0, 64, lo, hi), US[g][lo:hi, j, 0:64], CH[g][lo:hi, j, 0:64], True, True,
                                  [r_US[g], r_CH[g]], wb(g, j)))
                    calls.append((Wc(g, j, 64, 128, lo, hi), ident_b[lo:hi, lo:hi], AR[s][lo:hi, ck(g, j) * 128 + 64:(ck(g, j) + 1) * 128],
                                  True, False, [r_identb, r_AR[s]], wb(g, j)))
                    calls.append((Wc(g, j, 64, 128, lo, hi), US[g][lo:hi, j, 0:64], GM[g][lo:hi, j, 64:128], False, True,
                                  [r_US[g], r_GM[g]], wb(g, j)))
                for (lo, hi) in HALF:
                    calls.append((Nc(g, j, 0, 64, lo, hi), CH[g][lo:hi, j, 0:64], US[g][lo:hi, j, 64:128], True, False,
                                  [r_CH[g], r_US[g]], rN(g)))
                    calls.append((Nc(g, j, 0, 64, lo, hi), CH[g][lo:hi, j, 64:128], CH[g][lo:hi, j, 128:192], False, True,
                                  [r_CH[g]], rN(g)))
                grp(calls)
        for g in GRP:
            cp(g, PO[g][:], Wall(g, 0, 128), rW(g), [r_PO[g]])
            cp(g, QQ[g][:], Nall(g, 0, 64), rN(g), [r_QQ[g]])
        run_bg(bg, 5)
        for _ in bg:
            pass
        have_next = n + 1 < len(iters)
        if q == 0:
            memset("dve", SQ[:, 0, :], 0.0, [r_SQ[0]])
        else:
            S.op("dve", lambda: nc.vector.tensor_copy(out=SQ[:, 0, :], in_=SQ[:, 8, :]), [r_SQ[8]], [r_SQ[0]])

        def OO_calls(g, j):
            c = ck(g, j)
            calls = []
            for (lo, hi) in HALF:
                o_ = PS[lo:hi, 6 * 512 + c * 64: 6 * 512 + (c + 1) * 64]
                calls.append((o_, US[g][lo:hi, j, 64:128], GM[g][lo:hi, j, 64:128], True, False,
                              [r_US[g], r_GM[g]], [PB[6]]))
                calls.append((o_, CH[g][lo:hi, j, 128:192], GM[g][lo:hi, j, 192:256], False, False,
                              [r_CH[g], r_GM[g]], [PB[6]]))
                calls.append((o_, SQ[lo:hi, c, :], PO[g][lo:hi, j, 64:128], False, True,
                              [r_SQ[c], r_PO[g]], [PB[6]]))
            return calls

        prev = None
        for g in GRP:
            for j in range(NG):
                c = ck(g, j)
                k2 = c % 2
                stt(SWQ[k2][:], SQ[:, c, :], Wt[s][:, c * 64 + 63: c * 64 + 64], QQ[g][:, j, :], ALU.mult, ALU.add,
                    [r_SQ[c], r_Wt[s], r_QQ[g]], [r_SWQ[k2]])
                calls = []
                for (lo, hi) in HALF:
                    calls.append((Nc(g, j, 0, 64, lo, hi), PO[g][lo:hi, j, 0:64], SQ[lo:hi, c, :], True, True,
                                  [r_PO[g], r_SQ[c]], [NCH[g]]))
                grp(calls)
                tt("dve", SQ[:, c + 1, :], Nc(g, j, 0, 64), SWQ[k2][:], ALU.add, [NCH[g], r_SWQ[k2]], [r_SQ[c + 1]])
                if prev is not None:
                    grp(OO_calls(*prev))
                if have_next:
                    grp(TR_calls(n + 1, g, j))
                prev = (g, j)
                if have_next and c == NG:
                    act(CH[0][:, :, 0:256], Wall(0, 0, 256), AF.Copy, rW(0), [r_CH[0]])
        grp(OO_calls(*prev))
        S.op("dve", lambda: nc.vector.tensor_copy(out=OF[s][:], in_=bank(6)), [PB[6]], [r_OF[s]])
        if have_next:
            act(CH[1][:, :, 0:256], Wall(1, 0, 256), AF.Copy, rW(1), [r_CH[1]])

    for _ in C1_gen(0):
        pass
    for n in range(len(iters)):
        gens = []
        if n >= 1:
            gens.append(C3_gen(n - 1))
        if n + 1 < len(iters):
            gens.append(C1_gen(n + 1))
        bg = chain_gens(gens)
        C2(n, bg)
        for _ in bg:
            pass
        if n == 0:
            chk('C2first')
    for _ in C3_gen(len(iters) - 1):
        pass
    bankset[0] = list(range(8))
    chk('C')
    pop()
    push()
    MG = sb([128, 8, T], BF16, "MG")
    r_MG = [[R(f"mg{j}_{q}") for q in range(4)] for j in range(8)]
    push()
    more_wbufs(6)
    GA = [sb([128, 512], F32, f"GA{i}") for i in range(2)]; r_GA = [R("ga0"), R("ga1")]
    GBt = [sb([128, 512], F32, f"GB{i}") for i in range(2)]; r_GB = [R("gb0"), R("gb1")]
    MA = [sb([128, 512], F32, f"MA{i}") for i in range(2)]; r_MA = [R("ma0"), R("ma1")]
    MBt = [sb([128, 512], F32, f"MB{i}") for i in range(2)]; r_MB = [R("mb0"), R("mb1")]

    def ya_regions(q):
        return [r_YA[j][q] for j in range(8)]

    def yb_regions(q):
        return [r_YB[j][q] for j in range(8)]

    wn = (load_w(w_in, PG0), load_w(w_a, 0), load_w(w_in, PG0 + 1024), load_w(w_b, 0))
    for j in range(8):
        wga, wa_, wgb, wb_ = wn
        for q in range(4):
            k = q % 2
            b = nextbank()
            proj_quad(wga, q, b)
            act(GA[k][:], bank(b), AF.Sigmoid, [PB[b]], [r_GA[k]])
            b = nextbank()
            proj_quad(wa_, q, b, YA, ya_regions)
            tt("dve", MA[k][:], bank(b), GA[k][:], ALU.mult, [PB[b], r_GA[k]], [r_MA[k]])
            b = nextbank()
            proj_quad(wgb, q, b)
            act(GBt[k][:], bank(b), AF.Sigmoid, [PB[b]], [r_GB[k]])
            b = nextbank()
            proj_quad(wb_, q, b, YBv, yb_regions)
            tt("dve", MBt[k][:], bank(b), GBt[k][:], ALU.mult, [PB[b], r_GB[k]], [r_MB[k]])
            tt("pool", MG[:, j, q * 512:(q + 1) * 512], MA[k][:], MBt[k][:], ALU.add, [r_MA[k], r_MB[k]], [r_MG[j][q]])
            if q == 1 and j < 7:
                c1 = (j + 1) * 128
                wn = (load_w(w_in, PG0 + c1), load_w(w_a, c1), load_w(w_in, PG0 + 1024 + c1), load_w(w_b, c1))

    chk('D')
    drop_wbufs()
    pop()
    push()
    WO = sb([128, 8, 1024], BF16, "WO"); r_wv = R("wo")
    for kc in range(8):
        S.dma("pool", WO[:, kc, :], w_o[kc * 128:(kc + 1) * 128, :], [], [r_wv])
    OT = [sb([128, D], F32, f"OT{i}") for i in range(3)]; r_OT = [R(f"ot{i}") for i in range(3)]
    XT = [sb([128, D], F32, f"XTe{i}") for i in range(3)]; r_XT = [R(f"xte{i}") for i in range(3)]
    GBC = sb([128, 1024], F32, "GBCe")
    bcast_row(g_post, GBC, r_gbc)
    out_toks = []
    E_bank = {}

    def E_s1(i):
        k, k4 = i % 3, i % 4
        q = i // 4
        S.dma("sp", XT[k][:], x[i * 128:(i + 1) * 128, :], [], [r_XT[k]])
        b = nextbank(2)
        E_bank[i] = b
        for h in range(2):
            for kc in range(8):
                mm(bank(b + h), MG[:, kc, i * 128:(i + 1) * 128], WO[:, kc, h * 512:(h + 1) * 512],
                   start=(kc == 0), stop=(kc == 7), reads=[r_MG[kc][q], r_wv], writes=[PB[b + h]], sig=(kc == 7))
        st = STAT[k4]
        S.op("dve", lambda: nc.vector.bn_stats(out=st[:, 0:6], in_=bank(b)), [PB[b]], [r_STAT[k4]])
        S.op("dve", lambda: nc.vector.bn_stats(out=st[:, 6:12], in_=bank(b + 1)), [PB[b + 1]], [r_STAT[k4]])
        S.op("dve", lambda: nc.vector.bn_aggr(out=st[:, 12:14], in_=st[:, 0:12]), [r_STAT[k4]], [r_STAT[k4]])
        stt(st[:, 14:15], st[:, 12:13], st[:, 12:13], st[:, 13:14], ALU.mult, ALU.add, [r_STAT[k4]], [r_STAT[k4]])
        act(st[:, 14:15], st[:, 14:15], AF.Sqrt, [r_STAT[k4], r_eps], [r_STAT[k4]], bias=eps_rms[:], scale=1.0)

    def E_s2(i):
        k, k4 = i % 3, i % 4
        b = E_bank[i]
        st = STAT[k4]
        S.op("dve", lambda: nc.vector.reciprocal(out=st[:, 15:16], in_=st[:, 14:15]), [r_STAT[k4]], [r_STAT[k4]])
        for h in range(2):
            stt(OT[k][:, h * 512:(h + 1) * 512], bank(b + h), st[:, 15:16], GBC[:, h * 512:(h + 1) * 512], ALU.mult, ALU.mult,
                [PB[b + h], r_STAT[k4], r_gbc], [r_OT[k]])

    def E_s3(i):
        k = i % 3
        tt("pool", OT[k][:], OT[k][:], XT[k][:], ALU.add, [r_OT[k], r_XT[k]], [r_OT[k]])

    def E_s4(i):
        k = i % 3
        out_toks.append(S.dma("sp", y[i * 128:(i + 1) * 128, :], OT[k][:], [r_OT[k]], []))

    for step in range(16 + 3):
        if step < 16:
            E_s1(step)
        if 0 <= step - 1 < 16:
            E_s2(step - 1)
        if 0 <= step - 2 < 16:
            E_s3(step - 2)
        if 0 <= step - 3 < 16:
            E_s4(step - 3)
    for tok in out_toks:
        S.wait_tok("sp", tok)
    pop()
    pop()
    stacks.pop().close()


_CACHE = {}


def kernel(**inputs):
    if "nc" not in _CACHE:
        _CACHE["nc"] = build_program()[0]
    nc = _CACHE["nc"]
    f = lambda a: np.ascontiguousarray(np.asarray(a, dtype=np.float32))
    x = f(inputs["x"])
    common = {
        "norm_pre_g": f(inputs["norm_pre_g"]).reshape(1, D),
        "w_in": f(inputs["w_in"]).reshape(D, NCOL),
        "gm_ln_g": f(inputs["gm_ln_g"]).reshape(1, D),
        "gm_ln_b": f(inputs["gm_ln_b"]).reshape(1, D),
        "gm_w_s": f(inputs["gm_w_s"]).reshape(8, 128, 128),
        "gm_b_s": f(inputs["gm_b_s"]).reshape(1, 1024),
        "rw_mu": f(inputs["rw_mu"]).reshape(1, 4224),
        "rw_w0": f(inputs["rw_w0"]).reshape(1, D),
        "rw_decay_up": f(inputs["rw_decay_up"]).reshape(64, D),
        "rw_a0": f(inputs["rw_a0"]).reshape(1, D),
        "rw_iclr_up": f(inputs["rw_iclr_up"]).reshape(64, D),
        "rw_k_k": f(inputs["rw_k_k"]).reshape(1, D),
        "rw_k_a": f(inputs["rw_k_a"]).reshape(1, D),
        "rw_r_k": f(inputs["rw_r_k"]).reshape(1, D),
        "rw_gn_g": f(inputs["rw_gn_g"]).reshape(1, D),
        "rw_gn_b": f(inputs["rw_gn_b"]).reshape(1, D),
        "w_branch_a": f(inputs["w_branch_a"]).reshape(D, D),
        "w_branch_b": f(inputs["w_branch_b"]).reshape(D, D),
        "w_out": f(inputs["w_out"]).reshape(D, D),
        "norm_post_g": f(inputs["norm_post_g"]).reshape(1, D),
    }
    in_maps = []
    for b in range(8):
        m = dict(common)
        m["x"] = np.ascontiguousarray(x[b])
        in_maps.append(m)
    res = run_bass_kernel_spmd(nc, in_maps, core_ids=list(range(8)))
    out = np.stack([np.asarray(r["y"], dtype=np.float32) for r in res.results], axis=0)
    return out
```
